# Optimizing a Trainium2 kernel written in Bass

```python
import jax
import jax.numpy as jnp
from jax import lax
import numpy as np

D_MODEL = 1024
BATCH = 2
SEQ = 16384
DEPTH = 4
DEC_BATCH = 8
DEC_SEQ = 32
PAST_LEN = 1024

CHUNK = 64
N_EVEN = (DEPTH + 1) // 2
N_ODD = DEPTH // 2
D_FF = 2816
FFN_RES = 0.5
A_HEAD_DIM = 64
D_A = D_MODEL // 2
A_HEADS = D_A // A_HEAD_DIM
DECAY_RANK = 64
ICL_RANK = 64
GATE_RANK = 128
A_COLS = 3 * D_A + DECAY_RANK + ICL_RANK + GATE_RANK
D_B = D_MODEL - D_A
CONV_W = 3
B_COLS = 3 * D_B
EVEN_IN = A_COLS + B_COLS
D_EVEN_OUT = D_A + D_B
C_HEADS = 4
C_DK = D_MODEL // 2 // C_HEADS
C_DV = D_MODEL // C_HEADS
C_KEY = C_HEADS * C_DK
C_VAL = C_HEADS * C_DV
GLA_GATE_RANK = 16
GLA_GATE_NORM = 16.0
ODD_IN = 2 * C_KEY + 2 * C_VAL + GLA_GATE_RANK

RMS_EPS = 1e-6
GN_EPS = 64e-5
L2_EPS = 1e-12

kernel_name = 'hybrid_streaming_encoder_step'


def _rms(x, g):
    xf = x.astype(jnp.float32)
    xf = xf * lax.rsqrt(jnp.mean(xf * xf, axis=-1, keepdims=True) + RMS_EPS)
    return (xf * g.astype(jnp.float32)).astype(x.dtype)


def _swiglu(h, w_gate, w_up, w_down):
    a = jnp.einsum('btd,df->btf', h, w_gate)
    b = jnp.einsum('btd,df->btf', h, w_up)
    return jnp.einsum('btf,fd->btd', jax.nn.silu(a) * b, w_down)


def _rwkv7_scan(r, w, k, v, kk, a, s0):
    def step(s, inp):
        r_t, w_t, k_t, v_t, kk_t, a_t = inp
        sa = jnp.einsum('bhvk,bhk->bhv', s, kk_t)
        s = (s * w_t[:, :, None, :]
             - sa[..., None] * (kk_t * a_t)[:, :, None, :]
             + v_t[..., None] * k_t[:, :, None, :])
        y = jnp.einsum('bhvk,bhk->bhv', s, r_t)
        return s, y
    xs = tuple(jnp.moveaxis(z, 1, 0) for z in (r, w, k, v, kk, a))
    s_fin, ys = lax.scan(step, s0, xs)
    return jnp.moveaxis(ys, 0, 1), s_fin


def _gla_chunked(q, k, v, log_a, s0):
    bsz, t, nh, _ = q.shape
    dv = v.shape[-1]
    blk = CHUNK if t % CHUNK == 0 else t
    nc = t // blk

    def blocks(z):
        return z.reshape(bsz, nc, blk, nh, z.shape[-1]).transpose(1, 0, 3, 2, 4)

    qb, kb, vb, gb = blocks(q), blocks(k), blocks(v), blocks(log_a)
    cum = jnp.cumsum(gb, axis=3)
    last = cum[:, :, :, -1:, :]
    q_dec = qb * jnp.exp(cum)
    k_dec = kb * jnp.exp(-cum)
    k_to_end = kb * jnp.exp(last - cum)
    mask = jnp.tril(jnp.ones((blk, blk), dtype=bool))
    scores = jnp.where(mask, jnp.einsum('nbhtd,nbhsd->nbhts', q_dec, k_dec), 0.0)
    o_intra = jnp.einsum('nbhts,nbhsv->nbhtv', scores, vb)
    kv_blk = jnp.einsum('nbhsd,nbhsv->nbhdv', k_to_end, vb)
    blk_decay = jnp.exp(last[:, :, :, 0, :])

    def step(s, inp):
        q_c, kv_c, dec_c = inp
        o_inter = jnp.einsum('bhtd,bhdv->bhtv', q_c, s)
        s = s * dec_c[..., None] + kv_c
        return s, o_inter

    s_fin, o_inter = lax.scan(step, s0, (q_dec, kv_blk, blk_decay))
    o = (o_intra + o_inter).transpose(1, 0, 3, 2, 4).reshape(bsz, t, nh, dv)
    return o, s_fin


def _even_mixer(h, shift_buf, wkv0, conv_buf, p, i):
    bsz, t, _ = h.shape
    f32 = jnp.float32
    proj = jnp.einsum('btd,de->bte', h, p['ev_w_in'][i]).astype(f32)
    pa, pb = proj[..., :A_COLS], proj[..., A_COLS:]
    prev = jnp.concatenate([shift_buf.astype(f32)[:, None], pa[:, :-1]], axis=1)
    xs = pa + (prev - pa) * p['rwkv_mu'][i]
    r = xs[..., :D_A]
    k = xs[..., D_A:2 * D_A]
    v = xs[..., 2 * D_A:3 * D_A]
    o = 3 * D_A
    w_lo = xs[..., o:o + DECAY_RANK]
    o += DECAY_RANK
    a_lo = xs[..., o:o + ICL_RANK]
    o += ICL_RANK
    g_lo = xs[..., o:o + GATE_RANK]
    w_log = jax.nn.log_sigmoid(p['rwkv_w0'][i] + jnp.tanh(w_lo) @ p['rwkv_w_up'][i]) - 0.5
    decay = jnp.exp(-jnp.exp(w_log))
    a = jax.nn.sigmoid(p['rwkv_a0'][i] + a_lo @ p['rwkv_a_up'][i])
    g = jax.nn.sigmoid(g_lo) @ p['rwkv_g_up'][i]

    def heads(z):
        return z.reshape(bsz, t, A_HEADS, A_HEAD_DIM)

    kk = heads(k * p['rwkv_k_k'][i])
    kk = kk / jnp.maximum(jnp.sqrt(jnp.sum(kk * kk, axis=-1, keepdims=True)), L2_EPS)
    k_eff = heads(k * (1.0 + (a - 1.0) * p['rwkv_k_a'][i]))
    r_h, v_h = heads(r), heads(v)
    y, wkv_fin = _rwkv7_scan(r_h, heads(decay), k_eff, v_h, kk, heads(a), wkv0.astype(f32))
    mean = jnp.mean(y, axis=-1, keepdims=True)
    var = jnp.mean(jnp.square(y - mean), axis=-1, keepdims=True)
    y = ((y - mean) * lax.rsqrt(var + GN_EPS)).reshape(bsz, t, D_A) * p['rwkv_ln_w'][i] + p['rwkv_ln_b'][i]
    bonus = jnp.sum(r_h * k_eff * p['rwkv_r_k'][i], axis=-1, keepdims=True) * v_h
    y_a = (y + bonus.reshape(bsz, t, D_A)) * g
    gate_b, gate_c, hb = pb[..., :D_B], pb[..., D_B:2 * D_B], pb[..., 2 * D_B:]
    u = gate_c * hb
    padded = jnp.concatenate([conv_buf.astype(f32), u], axis=1)
    conv = lax.conv_general_dilated(padded, p['conv_w'][i].astype(f32)[:, None, :],
                                    window_strides=(1,), padding='VALID',
                                    dimension_numbers=('NWC', 'WIO', 'NWC'),
                                    feature_group_count=D_B)
    y_b = gate_b * conv
    out = jnp.einsum('bte,ed->btd', jnp.concatenate([y_a, y_b], axis=-1).astype(h.dtype), p['ev_w_out'][i])
    return out, pa[:, -1], wkv_fin, padded[:, -(CONV_W - 1):]


def _odd_mixer(h, gla0, p, i):
    bsz, t, _ = h.shape
    f32 = jnp.float32
    proj = jnp.einsum('btd,de->bte', h, p['od_w_in'][i]).astype(f32)
    q = proj[..., :C_KEY]
    k = proj[..., C_KEY:2 * C_KEY]
    v = proj[..., 2 * C_KEY:2 * C_KEY + C_VAL]
    g = proj[..., 2 * C_KEY + C_VAL:2 * C_KEY + 2 * C_VAL]
    a_lo = proj[..., 2 * C_KEY + 2 * C_VAL:]
    log_a = jax.nn.log_sigmoid(a_lo @ p['gla_a_up'][i] + p['gla_a_b'][i]) / GLA_GATE_NORM
    q = q.reshape(bsz, t, C_HEADS, C_DK) * (C_DK ** -0.5)
    k = k.reshape(bsz, t, C_HEADS, C_DK)
    v = v.reshape(bsz, t, C_HEADS, C_DV)
    log_a = log_a.reshape(bsz, t, C_HEADS, C_DK)
    o, gla_fin = _gla_chunked(q, k, v, log_a, gla0.astype(f32))
    o = o * lax.rsqrt(jnp.mean(o * o, axis=-1, keepdims=True) + RMS_EPS) * p['gla_norm'][i]
    o = o.reshape(bsz, t, C_VAL) * jax.nn.silu(g)
    out = jnp.einsum('bte,ed->btd', o.astype(h.dtype), p['od_w_out'][i])
    return out, gla_fin


def _trunk(x, shift0, wkv0, conv0, gla0, p):
    new_shift, new_wkv, new_conv, new_gla = [], [], [], []
    for layer in range(DEPTH):
        x = x + FFN_RES * _swiglu(_rms(x, p['ffn_norm'][layer, 0]), p['ffn_w_gate'][layer, 0],
                                  p['ffn_w_up'][layer, 0], p['ffn_w_down'][layer, 0])
        h = _rms(x, p['mix_norm'][layer])
        i = layer // 2
        if layer % 2 == 0:
            out, sh, wkv, cv = _even_mixer(h, shift0[i], wkv0[i], conv0[i], p, i)
            new_shift.append(sh)
            new_wkv.append(wkv)
            new_conv.append(cv)
        else:
            out, sg = _odd_mixer(h, gla0[i], p, i)
            new_gla.append(sg)
        x = x + out
        x = x + FFN_RES * _swiglu(_rms(x, p['ffn_norm'][layer, 1]), p['ffn_w_gate'][layer, 1],
                                  p['ffn_w_up'][layer, 1], p['ffn_w_down'][layer, 1])
    y = _rms(x, p['final_norm'])
    return y, jnp.stack(new_shift), jnp.stack(new_wkv), jnp.stack(new_conv), jnp.stack(new_gla)


def setup_inputs(seed: int = 0) -> dict:
    key = jax.random.key(seed)
    ks = iter(jax.random.split(key, 40))

    def nrm(shape, scale):
        return scale * jax.random.normal(next(ks), shape, jnp.float32)

    return {
        'x_prompt': nrm((BATCH, SEQ, D_MODEL), 1.0),
        'x_sample': nrm((DEC_BATCH, DEC_SEQ, D_MODEL), 1.0),
        'state_rwkv_shift': nrm((N_EVEN, DEC_BATCH, A_COLS), 1.0),
        'state_rwkv_wkv': nrm((N_EVEN, DEC_BATCH, A_HEADS, A_HEAD_DIM, A_HEAD_DIM), 0.3),
        'state_conv': nrm((N_EVEN, DEC_BATCH, CONV_W - 1, D_B), 0.5),
        'state_gla': nrm((N_ODD, DEC_BATCH, C_HEADS, C_DK, C_DV), 0.3),
        'ffn_norm': 1.0 + nrm((DEPTH, 2, D_MODEL), 0.02),
        'ffn_w_gate': nrm((DEPTH, 2, D_MODEL, D_FF), D_MODEL ** -0.5),
        'ffn_w_up': nrm((DEPTH, 2, D_MODEL, D_FF), D_MODEL ** -0.5),
        'ffn_w_down': nrm((DEPTH, 2, D_FF, D_MODEL), D_FF ** -0.5),
        'mix_norm': 1.0 + nrm((DEPTH, D_MODEL), 0.02),
        'ev_w_in': nrm((N_EVEN, D_MODEL, EVEN_IN), D_MODEL ** -0.5),
        'ev_w_out': nrm((N_EVEN, D_EVEN_OUT, D_MODEL), D_EVEN_OUT ** -0.5),
        'rwkv_mu': jax.random.uniform(next(ks), (N_EVEN, A_COLS), jnp.float32),
        'rwkv_w0': nrm((N_EVEN, D_A), 0.5),
        'rwkv_w_up': nrm((N_EVEN, DECAY_RANK, D_A), 0.3 * DECAY_RANK ** -0.5),
        'rwkv_a0': nrm((N_EVEN, D_A), 0.3),
        'rwkv_a_up': nrm((N_EVEN, ICL_RANK, D_A), 0.5 * ICL_RANK ** -0.5),
        'rwkv_g_up': nrm((N_EVEN, GATE_RANK, D_A), GATE_RANK ** -0.5),
        'rwkv_k_k': 0.85 + nrm((N_EVEN, D_A), 0.05),
        'rwkv_k_a': 1.0 + nrm((N_EVEN, D_A), 0.05),
        'rwkv_r_k': nrm((N_EVEN, A_HEADS, A_HEAD_DIM), 0.1),
        'rwkv_ln_w': 1.0 + nrm((N_EVEN, D_A), 0.02),
        'rwkv_ln_b': nrm((N_EVEN, D_A), 0.02),
        'conv_w': nrm((N_EVEN, CONV_W, D_B), CONV_W ** -0.5),
        'od_w_in': nrm((N_ODD, D_MODEL, ODD_IN), D_MODEL ** -0.5),
        'od_w_out': nrm((N_ODD, C_VAL, D_MODEL), C_VAL ** -0.5),
        'gla_a_up': nrm((N_ODD, GLA_GATE_RANK, C_KEY), GLA_GATE_RANK ** -0.5),
        'gla_a_b': nrm((N_ODD, C_KEY), 0.5),
        'gla_norm': 1.0 + nrm((N_ODD, C_DV), 0.02),
        'final_norm': 1.0 + nrm((D_MODEL,), 0.02),
    }


def reference(x_prompt, x_sample, state_rwkv_shift, state_rwkv_wkv, state_conv, state_gla,
              ffn_norm, ffn_w_gate, ffn_w_up, ffn_w_down, mix_norm,
              ev_w_in, ev_w_out, rwkv_mu, rwkv_w0, rwkv_w_up, rwkv_a0, rwkv_a_up, rwkv_g_up,
              rwkv_k_k, rwkv_k_a, rwkv_r_k, rwkv_ln_w, rwkv_ln_b, conv_w,
              od_w_in, od_w_out, gla_a_up, gla_a_b, gla_norm, final_norm):
    p = {
        'ffn_norm': ffn_norm, 'ffn_w_gate': ffn_w_gate, 'ffn_w_up': ffn_w_up, 'ffn_w_down': ffn_w_down,
        'mix_norm': mix_norm, 'ev_w_in': ev_w_in, 'ev_w_out': ev_w_out, 'rwkv_mu': rwkv_mu,
        'rwkv_w0': rwkv_w0, 'rwkv_w_up': rwkv_w_up, 'rwkv_a0': rwkv_a0, 'rwkv_a_up': rwkv_a_up,
        'rwkv_g_up': rwkv_g_up, 'rwkv_k_k': rwkv_k_k, 'rwkv_k_a': rwkv_k_a, 'rwkv_r_k': rwkv_r_k,
        'rwkv_ln_w': rwkv_ln_w, 'rwkv_ln_b': rwkv_ln_b, 'conv_w': conv_w,
        'od_w_in': od_w_in, 'od_w_out': od_w_out, 'gla_a_up': gla_a_up, 'gla_a_b': gla_a_b,
        'gla_norm': gla_norm, 'final_norm': final_norm,
    }
    bsz = x_prompt.shape[0]
    y_prompt, p_shift, p_wkv, p_conv, p_gla = _trunk(
        x_prompt,
        jnp.zeros((N_EVEN, bsz, A_COLS), jnp.float32),
        jnp.zeros((N_EVEN, bsz, A_HEADS, A_HEAD_DIM, A_HEAD_DIM), jnp.float32),
        jnp.zeros((N_EVEN, bsz, CONV_W - 1, D_B), jnp.float32),
        jnp.zeros((N_ODD, bsz, C_HEADS, C_DK, C_DV), jnp.float32),
        p)
    y_sample, s_shift, s_wkv, s_conv, s_gla = _trunk(
        x_sample, state_rwkv_shift, state_rwkv_wkv, state_conv, state_gla, p)
    return (y_prompt, y_sample, p_shift, p_wkv, p_conv, p_gla, s_shift, s_wkv, s_conv, s_gla)
```

```python
import math
from contextlib import ExitStack

import numpy as np
import concourse.bass as bass
import concourse.mybir as mybir
from concourse.bass_utils import run_bass_kernel_spmd

F32 = mybir.dt.float32
BF16 = mybir.dt.bfloat16
ALU = mybir.AluOpType
AF = mybir.ActivationFunctionType
PE, ACT, DVE, POOL, SP = "tensor", "scalar", "vector", "gpsimd", "sync"

D = 1024
DFF = 2816
NFC = 22
DEPTH = 4
A_COLS = 1792
EVEN_IN = 3328
ODD_IN = 3088
RMS_EPS = 1e-6
GN_EPS = 64e-5


class Key:
    __slots__ = ("name", "writer", "readers", "excl")

    def __init__(self, name="", excl=False):
        self.name = name
        self.writer = None
        self.readers = {}
        self.excl = excl


def _addr(readers, tok):
    cur = readers.get(tok[0])
    if cur is None or cur[1] < tok[1]:
        readers[tok[0]] = tok


class Sched:
    def __init__(self, nc, stack):
        self.nc = nc
        self.stack = stack
        self.engs = (PE, ACT, DVE, POOL, SP)
        self.ops = {e: [] for e in self.engs}
        self.count = {e: 0 for e in self.engs}
        self.sems = {e: stack.enter_context(nc.semaphore("s_" + e)) for e in self.engs}
        self.dma_sems = {}
        self.dma_count = {}
        self.final_sems = set()

    def sbuf(self, name, shape, dtype):
        return self.stack.enter_context(self.nc.sbuf_tensor(name, list(shape), dtype))

    def psum(self, name, shape, dtype):
        return self.stack.enter_context(self.nc.psum_tensor(name, list(shape), dtype))

    def _deps(self, eng, reads, writes):
        deps = []
        for k in reads:
            if k.writer is not None:
                deps.append(k.writer)
        same_ok = (eng == PE)
        for k in writes:
            w = k.writer
            if w is not None and (w[2] != eng or w[3] or not same_ok):
                deps.append(w)
            for r in k.readers.values():
                if r[2] != eng or r[3] or not same_ok:
                    deps.append(r)
        return deps

    def _commit(self, tok, reads, writes):
        for k in reads:
            _addr(k.readers, tok)
        for k in writes:
            k.writer = tok
            k.readers = {}

    @staticmethod
    def _excl(reads, writes):
        ex = [k for k in reads if k.excl]
        if not ex:
            return reads, writes
        return [k for k in reads if not k.excl], list(writes) + ex

    def op(self, eng, fn, reads=(), writes=()):
        reads, writes = self._excl(reads, writes)
        deps = self._deps(eng, reads, writes)
        self.count[eng] += 1
        tok = (eng, self.count[eng], eng, False)
        self.ops[eng].append((deps, fn, None))
        self._commit(tok, reads, writes)
        return tok

    def dma(self, queue, out, in_, sem, reads=(), writes=(), final=False):
        reads, writes = self._excl(reads, writes)
        deps = self._deps(queue, reads, writes)
        if sem is None:
            sem = f"u{len(self.dma_sems)}"
        if final:
            self.final_sems.add(sem)
        if sem not in self.dma_sems:
            self.dma_sems[sem] = self.stack.enter_context(self.nc.semaphore("d_" + sem))
            self.dma_count[sem] = 0
        self.dma_count[sem] += 16
        tok = ("dma:" + sem, self.dma_count[sem], queue, True)
        self.ops[queue].append((deps, lambda e: e.dma_start(out=out, in_=in_), sem))
        self._commit(tok, reads, writes)
        return tok

    def final_waits(self, queue, sems):
        deps = [("dma:" + s, self.dma_count[s], queue, True) for s in sems if s in self.dma_sems]
        self.ops[queue].append((deps, None, None))

    def _semh(self, sid):
        return self.dma_sems[sid[4:]] if sid.startswith("dma:") else self.sems[sid]

    def emit_engine(self, eng, handle):
        waited = {}
        for deps, fn, dsem in self.ops[eng]:
            need = {}
            for d in deps:
                if waited.get(d[0], 0) >= d[1]:
                    continue
                if need.get(d[0], 0) < d[1]:
                    need[d[0]] = d[1]
            for sid, val in need.items():
                handle.wait_ge(self._semh(sid), val)
                waited[sid] = val
            if fn is None:
                continue
            ins = fn(handle)
            if dsem is None:
                ins.then_inc(self.sems[eng], 1)
            else:
                ins.then_inc(self.dma_sems[dsem], 16)

    def emit(self):
        with self.nc.Block() as block:
            @block.tensor
            def _(e):
                self.emit_engine(PE, e)

            @block.scalar
            def _(e):
                self.emit_engine(ACT, e)

            @block.vector
            def _(e):
                self.emit_engine(DVE, e)

            @block.gpsimd
            def _(e):
                self.emit_engine(POOL, e)

            @block.sync
            def _(e):
                self.emit_engine(SP, e)


class Arena:
    def __init__(self, S, name, nbytes):
        self.t = S.sbuf(name, [128, nbytes // 4], F32)
        self.size = nbytes
        self.top = 0
        self.recs = []
        self.peak = 0

    def alloc(self, nbytes, name=""):
        nbytes = (nbytes + 63) // 64 * 64
        s, e = self.top, self.top + nbytes
        assert e <= self.size, f"arena overflow {name}: {e} > {self.size}"
        key = Key(name)
        keep = []
        for (rs, re_, rks) in self.recs:
            if rs < e and re_ > s:
                for rk in rks:
                    for r in rk.readers.values():
                        _addr(key.readers, r)
                    if rk.writer is not None:
                        _addr(key.readers, rk.writer)
                if rs >= s and re_ <= e:
                    continue
            keep.append((rs, re_, rks))
        rec = [key]
        keep.append((s, e, rec))
        self.recs = keep
        self.last_rec = rec
        self.top = e
        self.peak = max(self.peak, e)
        return self.t[:, s // 4:e // 4], key

    def subkeys(self, key, n):
        assert self.last_rec[0] is key
        ks = [Key(key.name + str(j)) for j in range(n)]
        for k_ in ks:
            k_.readers = dict(key.readers)
        self.last_rec[:] = ks
        return ks

    def f32(self, n, name=""):
        ap, k = self.alloc(n * 4, name)
        return ap[:, 0:n], k

    def bf16(self, n, name=""):
        ap, k = self.alloc(n * 2, name)
        return ap.bitcast(BF16)[:, 0:n], k

    def mark(self):
        return self.top

    def release(self, m):
        self.top = m


class Ring:
    def __init__(self, items):
        self.items = items
        self.i = 0

    def next(self):
        it = self.items[self.i % len(self.items)]
        self.i += 1
        return it


def _param_layout():
    off = {}
    r = 0

    def add(name, n):
        nonlocal r
        off[name] = r
        r += n
    add("ffn_norm", 64)
    add("mix_norm", 32)
    add("final_norm", 8)
    for i in range(2):
        for nm, n in (("mu", 14), ("w0", 4), ("a0", 4), ("k_k", 4), ("k_a", 4), ("r_k", 4),
                      ("ln_w", 4), ("ln_b", 4), ("conv_w", 12)):
            add(f"{nm}{i}", n)
    for i in range(2):
        add(f"gla_b{i}", 4)
        add(f"gla_n{i}", 2)
    return off, r


DBG = set()
POFF, PROWS = _param_layout()
NCONST = 128 * 5 + 512


def _pack_params(inp):
    rows = np.zeros((256, 128), np.float32)

    def put(name, arr):
        a = np.ascontiguousarray(arr, dtype=np.float32).reshape(-1, 128)
        rows[POFF[name]:POFF[name] + a.shape[0]] = a
    put("ffn_norm", inp["ffn_norm"])
    put("mix_norm", inp["mix_norm"])
    put("final_norm", inp["final_norm"])
    for i in range(2):
        put(f"mu{i}", inp["rwkv_mu"][i])
        put(f"w0{i}", inp["rwkv_w0"][i])
        put(f"a0{i}", inp["rwkv_a0"][i])
        put(f"k_k{i}", inp["rwkv_k_k"][i])
        put(f"k_a{i}", inp["rwkv_k_a"][i])
        put(f"r_k{i}", inp["rwkv_r_k"][i])
        put(f"ln_w{i}", inp["rwkv_ln_w"][i])
        put(f"ln_b{i}", inp["rwkv_ln_b"][i])
        put(f"conv_w{i}", inp["conv_w"][i])
        put(f"gla_b{i}", inp["gla_a_b"][i])
        put(f"gla_n{i}", inp["gla_norm"][i])
    return rows


def _consts():
    c = np.zeros((128, NCONST), np.float32)
    i = np.arange(128)[:, None]
    t = np.arange(128)[None, :]
    c[:, 0:128] = (i == t)
    c[:, 128:256] = (i < t)
    c[:, 256:384] = (i <= t)
    c[:, 384:512] = (i > t)
    c[:, 512:640] = ((i // 64) == (t // 64))
    rm = np.ones((128, 512), np.float32)
    rm[:, ::128] = 0.0
    c[:, 640:1152] = rm
    return c


def build(T, TS, NT=512, depth=DEPTH, arena_bytes=100 * 1024):
    nc = bass.Bass("TRN2", target_bir_lowering=False)
    n_even = (depth + 1) // 2
    n_odd = depth // 2

    def din(name, shape):
        return nc.dram_tensor(name, list(shape), F32, kind="ExternalInput").ap()

    def dout(name, shape):
        return nc.dram_tensor(name, list(shape), F32, kind="ExternalOutput").ap()

    def dscr(name, shape):
        return nc.dram_tensor(name, list(shape), BF16, kind="Internal").ap()

    xp = din("xp", [T, D])
    xs_in = din("xs", [TS, D])
    st_shift = din("st_shift", [2, A_COLS])
    st_wkv = din("st_wkv", [2, 8, 64, 64])
    st_conv = din("st_conv", [2, 2, 512])
    st_gla = din("st_gla", [2, 4, 128, 256])
    w_gate = din("ffn_w_gate", [depth, 2, D, DFF])
    w_up = din("ffn_w_up", [depth, 2, D, DFF])
    w_down = din("ffn_w_down", [depth, 2, DFF, D])
    ev_w_in = din("ev_w_in", [2, D, EVEN_IN])
    ev_w_out = din("ev_w_out", [2, D, D])
    od_w_in = din("od_w_in", [2, D, ODD_IN])
    od_w_out = din("od_w_out", [2, D, D])
    rwkv_w_up = din("rwkv_w_up", [2, 64, 512])
    rwkv_a_up = din("rwkv_a_up", [2, 64, 512])
    rwkv_g_up = din("rwkv_g_up", [2, 128, 512])
    gla_a_up = din("gla_a_up", [2, 16, 512])
    pvec = din("pvec", [256, 128])
    consts = din("consts", [128, NCONST])

    outs = {}
    for pre, in ("p", "s"):
        outs[pre + "_shift"] = dout(pre + "_shift", [2, A_COLS])
        outs[pre + "_wkv"] = dout(pre + "_wkv", [2, 8, 64, 64])
        outs[pre + "_conv"] = dout(pre + "_conv", [2, 2, 512])
        outs[pre + "_gla"] = dout(pre + "_gla", [2, 4, 128, 256])
    yp = dout("yp", [T, D])
    ys = dout("ys", [TS, D])

    sc_gu = dscr("sc_gu", [depth * 2, 11, 128, 2, 8, 256])
    sc_d = dscr("sc_d", [depth * 2, 8, 128, 11, 256])
    sc_evin = dscr("sc_evin", [2, 13, 128, 8, 256])
    sc_evout = dscr("sc_evout", [2, 4, 128, 8, 256])
    sc_odin = dscr("sc_odin", [2, 12, 128, 8, 256])
    sc_odtail = dscr("sc_odtail", [2, 128, 8, 16])
    sc_odout = dscr("sc_odout", [2, 4, 128, 8, 256])

    with ExitStack() as stack:
        S = Sched(nc, stack)

        cst = S.sbuf("cst", [128, NCONST], F32)
        k_cst = Key("cst")
        idf = cst[:, 0:128]
        cb = S.sbuf("cb", [128, 5, 128], BF16)
        k_cb = Key("cb")
        idb = cb[:, 0, :]
        blk1 = cb[:, 4, :]
        ones_b = S.sbuf("ones_b", [128, 128], BF16)
        k_ones = Key("ones")
        mask4 = S.sbuf("mask4", [128, 4, 128], F32)
        k_m4 = Key("m4")
        masku4 = S.sbuf("masku4", [128, 4, 128], F32)
        k_mu4 = Key("mu4")
        rmask = cst[:, 640:1152]
        msl = cst[:, 384:512]
        pv = S.sbuf("pv", [128, 256], F32)
        k_pv = Key("pv")
        pd = S.sbuf("pd", [128, 64], F32)
        k_pd = Key("pd")
        lw_up = S.sbuf("lw_up", [64, 2, 512], BF16)
        la_up = S.sbuf("la_up", [128, 2, 512], BF16)
        lg_up = S.sbuf("lg_up", [128, 2, 512], BF16)
        lgla_up = S.sbuf("lgla_up", [16, 2, 512], BF16)
        k_lr = Key("lowrank")

        xT = S.sbuf("xT", [128, 8, NT], F32)
        k_x = [Key(f"x{c}") for c in range(8)]
        hT = S.sbuf("hT", [128, 8, NT], BF16)
        k_h = [Key(f"h{c}") for c in range(8)]

        class SeqState:
            pass

        seqs = []
        for sname in ("p", "s"):
            ss = SeqState()
            ss.name = sname
            ss.ST = S.sbuf(sname + "ST", [128, 2, 4, 64], F32)
            ss.STb = S.sbuf(sname + "STb", [128, 2, 4, 64], BF16)
            ss.G = S.sbuf(sname + "G", [128, 2, 4, 256], F32)
            ss.Gb = S.sbuf(sname + "Gb", [128, 2, 4, 256], BF16)
            ss.shc = S.sbuf(sname + "shc", [128, 2, 14], F32)
            ss.cvc = S.sbuf(sname + "cvc", [128, 2, 4, 2], F32)
            ss.k_ST = [Key() for _ in range(2)]
            ss.k_STb = [Key() for _ in range(2)]
            ss.k_G = [[Key() for _ in range(2)] for _ in range(2)]
            ss.k_Gb = [[Key() for _ in range(2)] for _ in range(2)]
            ss.k_shc = [Key() for _ in range(2)]
            ss.k_cvc = [Key() for _ in range(2)]
            seqs.append(ss)

        def mkslots(name, n, shape):
            return Ring([(S.sbuf(f"{name}{i}", shape, BF16), Key(f"{name}{i}"), f"{name}{i}") for i in range(n)])
        sl_gu = mkslots("wgu", 2, [128, 2, 8, 256])
        sl_d = mkslots("wd", 3, [128, 11, 256])
        sl_p = mkslots("wp", 2, [128, 8, 256])
        sl_t = mkslots("wt", 1, [128, 8, 16])

        psF = Ring([(S.psum(f"psF{i}", [128, 512], F32), Key(f"psF{i}", excl=True)) for i in range(6)])
        psB = Ring([(S.psum(f"psB{i}", [128, 1024], BF16), Key(f"psB{i}", excl=True)) for i in range(2)])

        AR = Arena(S, "arena", (nc.sbuf_bytes_remaining - 1024) // 64 * 64)

        def mm(out, lhsT, rhs, start, stop, reads, writes):
            S.op(PE, lambda e: e.matmul(out, lhsT, rhs, start=start, stop=stop), reads, writes)

        def tr(out, in_, ident, reads, writes):
            S.op(PE, lambda e: e.transpose(out, in_, ident), reads, writes)

        def act(out, in_, func, reads, writes, bias=0.0, scale=1.0):
            S.op(ACT, lambda e: e.activation(out=out, in_=in_, func=func, bias=bias, scale=scale), reads, writes)

        def acopy(out, in_, reads, writes):
            S.op(ACT, lambda e: e.copy(out, in_), reads, writes)

        def amul(out, in_, m, reads, writes):
            S.op(ACT, lambda e: e.mul(out, in_, m), reads, writes)

        def tt(eng, out, in0, in1, op, reads, writes):
            S.op(eng, lambda e: e.tensor_tensor(out, in0, in1, op), reads, writes)

        def ts(eng, out, in0, s1, s2, op0, op1, reads, writes):
            if s2 is None:
                S.op(eng, lambda e: e.tensor_scalar(out, in0, s1, None, op0), reads, writes)
            else:
                S.op(eng, lambda e: e.tensor_scalar(out, in0, s1, s2, op0, op1), reads, writes)

        def stt(out, in0, sc, in1, op0, op1, reads, writes):
            S.op(DVE, lambda e: e.scalar_tensor_tensor(out, in0, sc, in1, op0, op1), reads, writes)

        def cp(eng, out, in_, reads, writes):
            S.op(eng, lambda e: e.tensor_copy(out, in_), reads, writes)

        def recip(out, in_, reads, writes):
            S.op(DVE, lambda e: e.reciprocal(out, in_), reads, writes)

        def mset(eng, ap, v, writes):
            S.op(eng, lambda e: e.memset(ap, v), (), writes)

        def pcol(name, j=0):
            r = POFF[name] + j
            return pv[:, r:r + 1]

        S.dma(SP, cst[:], consts[:, :], None, writes=[k_cst])
        for m in range(5):
            cp(DVE, cb[:, m, :], cst[:, m * 128:(m + 1) * 128], [k_cst], [k_cb])
        mset(POOL, ones_b[:], 1.0, [k_ones])
        for m in range(4):
            src = cst[:, 128:256] if m % 2 == 0 else cst[:, 256:384]
            cp(POOL, mask4[:, m, :], src, [k_cst], [k_m4])
            cp(POOL, masku4[:, m, :], cst[:, 256:384], [k_cst], [k_mu4])
        with_tmp = AR.mark()
        for half in range(2):
            stg, k_stg = AR.f32(128, "pstg")
            S.dma(SP, stg, pvec[half * 128:(half + 1) * 128, :], None, writes=[k_stg])
            ps, kps = psF.next()
            tr(ps[:, 0:128], stg, idf, [k_stg, k_cst], [kps])
            acopy(pv[:, half * 128:(half + 1) * 128], ps[:, 0:128], [kps], [k_pv])
        AR.release(with_tmp)
        for i in range(2):
            r = POFF[f"mu{i}"]
            ts(DVE, pd[:, i * 14:(i + 1) * 14], pv[:, r:r + 14], -1.0, 1.0, ALU.mult, ALU.add, [k_pv], [k_pd])
            r = POFF[f"gla_b{i}"]
            ts(DVE, pd[:, 32 + i * 4:32 + (i + 1) * 4], pv[:, r:r + 4], -1.0, None, ALU.mult, None, [k_pv], [k_pd])
        S.dma(POOL, lw_up[:], rwkv_w_up.rearrange("i k n -> k i n"), "c1", writes=[k_lr])
        S.dma(POOL, la_up[64:128], rwkv_a_up.rearrange("i k n -> k i n"), "c1", writes=[k_lr])
        S.dma(POOL, lg_up[:], rwkv_g_up.rearrange("i k n -> k i n"), "c1", writes=[k_lr])
        S.dma(POOL, lgla_up[:], gla_a_up.rearrange("i k n -> k i n"), "c1", writes=[k_lr])
        k_lr.writer = ("dma:c1", S.dma_count["c1"], POOL, True)

        wkeys = {}

        def conv_dma(layer, tag, out_ap, in_ap):
            k = Key(tag)
            S.dma(POOL, out_ap, in_ap, f"cv{layer}", writes=[k])
            wkeys.setdefault(layer, []).append(k)
            return k

        k_gu, k_d, k_evin, k_evout, k_odin, k_odt, k_odout = {}, {}, {}, {}, {}, {}, {}

        def conv_ffn(layer, j):
            f = layer * 2 + j
            for fb in range(11):
                ka = conv_dma(layer, "gu", sc_gu[f, fb, :, 0], w_gate[layer, j][:, fb * 256:(fb + 1) * 256].rearrange("(kc p) w -> p kc w", p=128))
                kb = conv_dma(layer, "gu", sc_gu[f, fb, :, 1], w_up[layer, j][:, fb * 256:(fb + 1) * 256].rearrange("(kc p) w -> p kc w", p=128))
                k_gu[(f, fb)] = [ka, kb]
            for db in range(4):
                for fh in range(2):
                    k_d[(f, db * 2 + fh)] = [conv_dma(layer, "d", sc_d[f, db * 2 + fh],
                                                      w_down[layer, j][fh * 1408:(fh + 1) * 1408, db * 256:(db + 1) * 256].rearrange("(fc p) w -> p fc w", p=128))]

        def conv_proj(layer, dst, src, nb, kd, i):
            for b in range(nb):
                kd[(i, b)] = [conv_dma(layer, "p", dst[i, b], src[i][:, b * 256:(b + 1) * 256].rearrange("(kc p) w -> p kc w", p=128))]

        for layer in range(depth if "no_cv" not in DBG else 0):
            i = layer // 2
            conv_ffn(layer, 0)
            if layer % 2 == 0:
                conv_proj(layer, sc_evin, ev_w_in, 13, k_evin, i)
                conv_proj(layer, sc_evout, ev_w_out, 4, k_evout, i)
            else:
                conv_proj(layer, sc_odin, od_w_in, 12, k_odin, i)
                k_odt[i] = [conv_dma(layer, "pt", sc_odtail[i], od_w_in[i][:, 3072:3088].rearrange("(kc p) w -> p kc w", p=128))]
                conv_proj(layer, sc_odout, od_w_out, 4, k_odout, i)
            conv_ffn(layer, 1)
        for layer, ks in wkeys.items():
            fin = ("dma:" + f"cv{layer}", S.dma_count[f"cv{layer}"], POOL, True)
            for k in ks:
                k.writer = fin

        def wload(ring, dram_ap, keys):
            t, k, sem = ring.next()
            S.dma(SP, t[:], dram_ap, sem, reads=keys, writes=[k])
            return t, k

        for ss in seqs:
            if ss.name == "p" or "no_sload" in DBG:
                for i in range(2):
                    mset(POOL, ss.ST[:, i], 0.0, [ss.k_ST[i]])
                    mset(POOL, ss.STb[:, i], 0.0, [ss.k_STb[i]])
                    for hh in range(2):
                        mset(POOL, ss.G[:, i, 2 * hh:2 * hh + 2], 0.0, [ss.k_G[i][hh]])
                        mset(POOL, ss.Gb[:, i, 2 * hh:2 * hh + 2], 0.0, [ss.k_Gb[i][hh]])
                    mset(POOL, ss.shc[:, i], 0.0, [ss.k_shc[i]])
                    mset(POOL, ss.cvc[:, i], 0.0, [ss.k_cvc[i]])
            else:
                m0 = AR.mark()
                for i in range(n_even):
                    stg, k_stg = AR.f32(128, "sstg")
                    S.dma(SP, stg[0:14, :], st_shift[i].rearrange("(c p) -> c p", p=128), None, writes=[k_stg])
                    ps, kps = psF.next()
                    tr(ps[:, 0:14], stg[0:14, :], idf[0:14, 0:14], [k_stg, k_cst], [kps])
                    acopy(ss.shc[:, i], ps[:, 0:14], [kps], [ss.k_shc[i]])
                    stg2, k_stg2 = AR.f32(128, "cstg")
                    S.dma(SP, stg2[0:8, :], st_conv[i].rearrange("r (c p) -> (r c) p", p=128), None, writes=[k_stg2])
                    ps, kps = psF.next()
                    tr(ps[:, 0:8], stg2[0:8, :], idf[0:8, 0:8], [k_stg2, k_cst], [kps])
                    acopy(ss.cvc[:, i].rearrange("p c r -> p r c"), ps[:, 0:8].rearrange("p (r c) -> p r c", c=4), [kps], [ss.k_cvc[i]])
                    for hp in range(4):
                        stg3, k_stg3 = AR.f32(128, "wstg")
                        S.dma(SP, stg3[0:64, :].rearrange("v (h k) -> v h k", h=2), st_wkv[i, 2 * hp:2 * hp + 2].rearrange("h v k -> v h k"), None, writes=[k_stg3])
                        ps, kps = psF.next()
                        tr(ps[:, 0:64], stg3[0:64, :], idf[0:64, 0:64], [k_stg3, k_cst], [kps])
                        acopy(ss.ST[:, i, hp, :], ps[:, 0:64], [kps], [ss.k_ST[i]])
                    cp(POOL, ss.STb[:, i], ss.ST[:, i], [ss.k_ST[i]], [ss.k_STb[i]])
                for i in range(n_odd):
                    for hh in range(2):
                        S.dma(SP, ss.G[:, i, 2 * hh:2 * hh + 2, :], st_gla[i, 2 * hh:2 * hh + 2].rearrange("h k v -> k h v"), None, writes=[ss.k_G[i][hh]])
                        cp(POOL, ss.Gb[:, i, 2 * hh:2 * hh + 2, :], ss.G[:, i, 2 * hh:2 * hh + 2, :], [ss.k_G[i][hh]], [ss.k_Gb[i][hh]])
                AR.release(m0)

        def cbuf(nchunk, width, dtype, name):
            eb = 2 if dtype == BF16 else 4
            ap, kall = AR.alloc(nchunk * width * eb, name)
            if dtype == BF16:
                ap = ap.bitcast(BF16)
            v = ap[:, 0:nchunk * width].rearrange("p (c t) -> p c t", t=width)
            return v, AR.subkeys(kall, nchunk)

        def rmsnorm(n, gname, gbase):
            m0 = AR.mark()
            ps, kps = psF.next()
            sqr = [AR.bf16(n, "sq") for _ in range(2)]
            for c in range(8):
                sq, ksq = sqr[c % 2]
                act(sq, xT[:, c, 0:n], AF.Square, [k_x[c]], [ksq])
                mm(ps[:, 0:n], ones_b[:], sq, c == 0, c == 7, [ksq, k_ones], [kps])
            sd, ksd = AR.f32(n, "sd")
            act(sd, ps[:, 0:n], AF.Sqrt, [kps], [ksd], bias=RMS_EPS, scale=1.0 / D)
            rs, krs = AR.f32(n, "rs")
            recip(rs, sd, [ksd], [krs])
            for c in range(8):
                stt(hT[:, c, 0:n], xT[:, c, 0:n], pcol(gname, gbase + c), rs, ALU.mult, ALU.mult, [k_x[c], krs, k_pv], [k_h[c]])
            AR.release(m0)

        def ffn(n, layer, j):
            f = layer * 2 + j
            rmsnorm(n, "ffn_norm", f * 8)
            m0 = AR.mark()
            gT_ap, kgall = AR.alloc(NFC * n * 2, "gT")
            gT = gT_ap.bitcast(BF16)[:, 0:NFC * n].rearrange("p (c t) -> p c t", t=n)
            k_g = AR.subkeys(kgall, NFC)
            sar = Ring([AR.f32(n, "sA") for _ in range(2)])
            for fb in range(11):
                w, kw = wload(sl_gu, sc_gu[f, fb], k_gu[(f, fb)])
                for sub in range(2):
                    fc = fb * 2 + sub
                    psa, kpa = psF.next()
                    psb_, kpb = psF.next()
                    for kc in range(8):
                        mm(psa[:, 0:n], w[:, 0, kc, sub * 128:(sub + 1) * 128], hT[:, kc, 0:n], kc == 0, kc == 7, [kw, k_h[kc]], [kpa])
                    for kc in range(8):
                        mm(psb_[:, 0:n], w[:, 1, kc, sub * 128:(sub + 1) * 128], hT[:, kc, 0:n], kc == 0, kc == 7, [kw, k_h[kc]], [kpb])
                    sA, ksA = sar.next()
                    act(sA, psa[:, 0:n], AF.Silu, [kpa], [ksA])
                    tt(DVE, gT[:, fc, :], psb_[:, 0:n], sA, ALU.mult, [kpb, ksA], [k_g[fc]])
            for db in range(4):
                w0_, kw0 = wload(sl_d, sc_d[f, db * 2], k_d[(f, db * 2)])
                w1_, kw1 = wload(sl_d, sc_d[f, db * 2 + 1], k_d[(f, db * 2 + 1)])
                for sub in range(2):
                    dc = db * 2 + sub
                    ps, kps = psF.next()
                    for fc in range(NFC):
                        w_, kw_ = (w0_, kw0) if fc < 11 else (w1_, kw1)
                        mm(ps[:, 0:n], w_[:, fc % 11, sub * 128:(sub + 1) * 128], gT[:, fc, :], fc == 0, fc == NFC - 1, [kw_, k_g[fc]], [kps])
                    stt(xT[:, dc, 0:n], ps[:, 0:n], 0.5, xT[:, dc, 0:n], ALU.mult, ALU.add, [kps, k_x[dc]], [k_x[dc]])
            AR.release(m0)

        def out_proj(n, scr, kd, i, ycat, k_y):
            for db in range(4):
                w, kw = wload(sl_p, scr[i, db], kd[(i, db)])
                for sub in range(2):
                    dc = db * 2 + sub
                    ps, kps = psF.next()
                    for kc in range(8):
                        mm(ps[:, 0:n], w[:, kc, sub * 128:(sub + 1) * 128], ycat[:, kc, :], kc == 0, kc == 7, [kw, k_y[kc]], [kps])
                    tt(DVE, xT[:, dc, 0:n], ps[:, 0:n], xT[:, dc, 0:n], ALU.add, [kps, k_x[dc]], [k_x[dc]])

        def chunked(ap3, n):
            return ap3

        def even_mixer(ss, n, C, layer):
            i = layer // 2
            nck = n // C
            J = int(math.log2(C)) - 1
            rmsnorm(n, "mix_norm", layer * 8)
            m_base = AR.mark()
            ycat, k_y = cbuf(8, n, BF16, "ycat")
            th, k_th = AR.bf16(n, "th")
            alo, k_alo = AR.bf16(n, "alo")
            sg, k_sg = AR.bf16(n, "sg")
            rT, k_rT = cbuf(4, n, BF16, "rT")
            kaT, k_kaT = cbuf(4, n, BF16, "kaT")
            bT, k_bT = cbuf(4, n, BF16, "bT")
            kT, k_kT = cbuf(4, n, BF16, "kT")
            vb, k_vb = cbuf(4, n, BF16, "vb")
            bon, k_bon = cbuf(4, n, BF16, "bon")
            gg, k_gg = cbuf(4, n, BF16, "gg")
            gam_ap, k_gam = AR.f32(4 * nck, "gam")
            gam = gam_ap.rearrange("p (c k) -> p c k", k=nck)
            yT_ap, k_yTall = AR.alloc(4 * n * 4, "yT")
            yT = yT_ap[:, 0:4 * n].rearrange("p (c t) -> p c t", t=n)
            k_yT = AR.subkeys(k_yTall, nck)
            m_xs = AR.mark()
            mu0 = POFF[f"mu{i}"]

            def proj_chunk(eb):
                w, kw = wload(sl_p, sc_evin[i, eb], k_evin[(i, eb)])
                res = []
                for sub in range(2):
                    ps, kps = psF.next()
                    for kc in range(8):
                        mm(ps[:, 0:n], w[:, kc, sub * 128:(sub + 1) * 128], hT[:, kc, 0:n], kc == 0, kc == 7, [kw, k_h[kc]], [kps])
                    res.append((eb * 2 + sub, ps, kps))
                return res

            pbuf, k_pb = cbuf(8, n, F32, "pb")
            u, k_u = cbuf(4, n + 2, F32, "u")
            tmr = Ring([AR.f32(n, "tm1") for _ in range(3)])
            for eb in range(7, 13):
                for ec, ps, kps in proj_chunk(eb):
                    if ec < 22:
                        acopy(pbuf[:, ec - 14, :], ps[:, 0:n], [kps], [k_pb[ec - 14]])
                    else:
                        c = ec - 22
                        cp(POOL, u[:, c, 0:2], ss.cvc[:, i, c, :], [ss.k_cvc[i]], [k_u[c]])
                        tt(DVE, u[:, c, 2:n + 2], ps[:, 0:n], pbuf[:, 4 + c, :], ALU.mult, [kps, k_pb[4 + c]], [k_u[c]])
            cw = POFF[f"conv_w{i}"]
            for c in range(4):
                t1, kt1 = tmr.next()
                ts(POOL, t1, u[:, c, 0:n], pv[:, cw + c:cw + c + 1], None, ALU.mult, None, [k_u[c], k_pv], [kt1])
                t2, kt2 = tmr.next()
                stt(t2, u[:, c, 1:n + 1], pv[:, cw + 4 + c:cw + 5 + c], t1, ALU.mult, ALU.add, [k_u[c], kt1, k_pv], [kt2])
                t3, kt3 = tmr.next()
                stt(t3, u[:, c, 2:n + 2], pv[:, cw + 8 + c:cw + 9 + c], t2, ALU.mult, ALU.add, [k_u[c], kt2, k_pv], [kt3])
                tt(POOL, ycat[:, 4 + c, :], t3, pbuf[:, c, :], ALU.mult, [kt3, k_pb[c]], [k_y[4 + c]])
                cp(POOL, ss.cvc[:, i, c, :], u[:, c, n:n + 2], [k_u[c]], [ss.k_cvc[i]])
            AR.release(m_xs)
            xs, k_xs = cbuf(12, n, F32, "xs")
            m_a = AR.mark()
            par = Ring([AR.f32(n + 1, "pa1") for _ in range(2)])
            tmr = Ring([AR.f32(n, "tm2") for _ in range(3)])
            for eb in range(0, 7):
                for ec, ps, kps in proj_chunk(eb):
                    pa1, kpa1 = par.next()
                    cp(POOL, pa1[:, 0:1], ss.shc[:, i, ec:ec + 1], [ss.k_shc[i]], [kpa1])
                    acopy(pa1[:, 1:n + 1], ps[:, 0:n], [kps], [kpa1])
                    t1, kt1 = tmr.next()
                    ts(POOL, t1, pa1[:, 0:n], pv[:, mu0 + ec:mu0 + ec + 1], None, ALU.mult, None, [kpa1, k_pv], [kt1])
                    cp(POOL, ss.shc[:, i, ec:ec + 1], pa1[:, n:n + 1], [kpa1], [ss.k_shc[i]])
                    omm = pd[:, i * 14 + ec:i * 14 + ec + 1]
                    if ec < 12:
                        stt(xs[:, ec, :], pa1[:, 1:n + 1], omm, t1, ALU.mult, ALU.add, [kpa1, kt1, k_pd], [k_xs[ec]])
                    else:
                        t2, kt2 = tmr.next()
                        stt(t2, pa1[:, 1:n + 1], omm, t1, ALU.mult, ALU.add, [kpa1, kt1, k_pd], [kt2])
                        if ec == 12:
                            act(th[0:64, :], t2[0:64, :], AF.Tanh, [kt2], [k_th])
                            cp(POOL, alo[64:128, :], t2[64:128, :], [kt2], [k_alo])
                        else:
                            act(sg, t2, AF.Sigmoid, [kt2], [k_sg])
            AR.release(m_a)
            if "ev1" in DBG:
                for hp in range(4):
                    mset(POOL, ycat[:, hp, :], 0.0, [k_y[hp]])
                out_proj(n, sc_evout, k_evout, i, ycat, k_y)
                AR.release(m_base)
                return
            m_prep = AR.mark()
            P_w0, P_a0 = POFF[f"w0{i}"], POFF[f"a0{i}"]
            P_kk, P_ka, P_rk = POFF[f"k_k{i}"], POFF[f"k_a{i}"], POFF[f"r_k{i}"]
            for hp in range(4):
                AR.release(m_prep)
                xr, xk, xv = xs[:, hp, :], xs[:, 4 + hp, :], xs[:, 8 + hp, :]
                kxr, kxk, kxv = k_xs[hp], k_xs[4 + hp], k_xs[8 + hp]
                ps, kps = psF.next()
                mm(ps[:, 0:n], lw_up[0:64, i, hp * 128:(hp + 1) * 128], th[0:64, :], True, True, [k_lr, k_th], [kps])
                lw, klw = AR.f32(n, "lw")
                act(lw, ps[:, 0:n], AF.Sigmoid, [kps, k_pv], [klw], bias=pv[:, P_w0 + hp:P_w0 + hp + 1])
                ts(POOL, lw, lw, -math.exp(-0.5), None, ALU.mult, None, [klw], [klw])
                ps, kps = psF.next()
                mm(ps[:, 0:n], la_up[64:128, i, hp * 128:(hp + 1) * 128], alo[64:128, :], True, True, [k_lr, k_alo], [kps])
                aa, kaa = AR.f32(n, "aa")
                act(aa, ps[:, 0:n], AF.Sigmoid, [kps, k_pv], [kaa], bias=pv[:, P_a0 + hp:P_a0 + hp + 1])
                ps, kps = psF.next()
                mm(ps[:, 0:n], lg_up[:, i, hp * 128:(hp + 1) * 128], sg, True, True, [k_lr, k_sg], [kps])
                acopy(gg[:, hp, :], ps[:, 0:n], [kps], [k_gg[hp]])
                kk2, kkk2 = AR.bf16(n, "kk2")
                act(kk2, xk, AF.Square, [kxk, k_pv], [kkk2], scale=pv[:, P_kk + hp:P_kk + hp + 1])
                ps, kps = psF.next()
                mm(ps[:, 0:n], blk1, kk2, True, True, [k_cb, kkk2], [kps])
                rn, krn = AR.f32(n, "rn")
                act(rn, ps[:, 0:n], AF.Sqrt, [kps], [krn])
                ts(DVE, rn, rn, 1e-12, None, ALU.max, None, [krn], [krn])
                recip(rn, rn, [krn], [krn])
                kap, kkap = rn, krn
                stt(kap, xk, pv[:, P_kk + hp:P_kk + hp + 1], rn, ALU.mult, ALU.mult, [kxk, krn, k_pv], [kkap])
                t1, kt1 = AR.f32(n, "t1")
                ts(POOL, t1, aa, -1.0, pv[:, P_ka + hp:P_ka + hp + 1], ALU.add, ALU.mult, [kaa, k_pv], [kt1])
                kef, kkef = t1, kt1
                stt(kef, t1, 1.0, xk, ALU.add, ALU.mult, [kt1, kxk], [kkef])
                bb, kbb = aa, kaa
                tt(POOL, bb, kap, aa, ALU.mult, [kkap, kaa], [kbb])
                rk, krk = AR.bf16(n, "rk")
                stt(rk, xr, pv[:, P_rk + hp:P_rk + hp + 1], kef, ALU.mult, ALU.mult, [kxr, kkef, k_pv], [krk])
                ps, kps = psF.next()
                mm(ps[:, 0:n], blk1, rk, True, True, [k_cb, krk], [kps])
                tt(DVE, bon[:, hp, :], ps[:, 0:n], xv, ALU.mult, [kps, kxv], [k_bon[hp]])
                cp(POOL, vb[:, hp, :], xv, [kxv], [k_vb[hp]])
                cum, kcum = AR.f32(n, "cum")
                S.op(DVE, (lambda o, a_, b_: (lambda e: e.tensor_tensor_scan(o, a_, b_, 0.0, ALU.mult, ALU.add)))(cum, rmask[:, 0:n], lw),
                     [k_cst, klw], [kcum])
                epos, kep = AR.f32(n, "epos")
                act(epos, cum, AF.Exp, [kcum], [kep])
                eneg, ken = AR.f32(n, "eneg")
                act(eneg, cum, AF.Exp, [kcum], [ken], scale=-1.0)
                cml, kcml = lw, klw
                tt(POOL, cml, cum, lw, ALU.subtract, [kcum, klw], [kcml])
                act(cml, cml, AF.Exp, [kcml], [kcml])
                tt(DVE, rT[:, hp, :], xr, epos, ALU.mult, [kxr, kep], [k_rT[hp]])
                tt(POOL, kaT[:, hp, :], kap, cml, ALU.mult, [kkap, kcml], [k_kaT[hp]])
                tt(DVE, bT[:, hp, :], bb, eneg, ALU.mult, [kbb, ken], [k_bT[hp]])
                tt(POOL, kT[:, hp, :], kef, eneg, ALU.mult, [kkef, ken], [k_kT[hp]])
                cp(POOL, gam[:, hp, :], epos.rearrange("p (k t) -> p k t", t=C)[:, :, C - 1], [kep], [k_gam])
            AR.release(m_xs)
            if "ev2" in DBG:
                for hp in range(4):
                    mset(POOL, ycat[:, hp, :], 0.0, [k_y[hp]])
                out_proj(n, sc_evout, k_evout, i, ycat, k_y)
                AR.release(m_base)
                return
            for ck in range(nck):
                cs = slice(ck * C, (ck + 1) * C)
                m_ck = AR.mark()
                tm, k_tm = [], []
                for hp in range(4):
                    psb_, kpsb = psB.next()
                    tr(psb_[0:C, 0:128], vb[:, hp, cs], idb, [k_vb[hp], k_cb], [kpsb])
                    tr(psb_[0:C, 128:256], bT[:, hp, cs], idb, [k_bT[hp], k_cb], [kpsb])
                    tr(psb_[0:C, 256:384], kT[:, hp, cs], idb, [k_kT[hp], k_cb], [kpsb])
                    t_, kt_ = AR.bf16(384, "tm")
                    if hp % 2 == 0:
                        cp(DVE, t_[0:C, :], psb_[0:C, 0:384], [kpsb], [kt_])
                    else:
                        acopy(t_[0:C, :], psb_[0:C, 0:384], [kpsb], [kt_])
                    tm.append(t_)
                    k_tm.append(kt_)
                sc, k_sc, Q, k_Q, R, k_R, Pm, k_P = [], [], [], [], [], [], [], []
                Rpp, PQpp = [], []
                for h in range(8):
                    hp, hl = h // 2, h % 2
                    pr = slice(64 * hl, 64 * hl + 64)
                    ps, kps = psF.next()
                    psv = ps[0:C, :].rearrange("p (m t) -> p m t", t=128)
                    mm(psv[:, 0, 0:C], bT[pr, hp, cs], kaT[pr, hp, cs], True, True, [k_bT[hp], k_kaT[hp]], [kps])
                    mm(psv[:, 1, 0:C], bT[pr, hp, cs], rT[pr, hp, cs], True, True, [k_bT[hp], k_rT[hp]], [kps])
                    mm(psv[:, 2, 0:C], kT[pr, hp, cs], kaT[pr, hp, cs], True, True, [k_kT[hp], k_kaT[hp]], [kps])
                    mm(psv[:, 3, 0:C], kT[pr, hp, cs], rT[pr, hp, cs], True, True, [k_kT[hp], k_rT[hp]], [kps])
                    s_, ks_ = AR.bf16(4 * C, "sc")
                    s3 = s_[0:C, :].rearrange("p (m t) -> p m t", t=C)
                    tt(DVE, s3, psv[:, :, 0:C], mask4[0:C, :, 0:C], ALU.mult, [kps, k_m4], [ks_])
                    sc.append(s3)
                    k_sc.append(ks_)
                    ps2, kps2 = psF.next()
                    mm(ps2[0:C, 0:C], kaT[pr, hp, cs], bT[pr, hp, cs], True, True, [k_kaT[hp], k_bT[hp]], [kps2])
                    q_, kq_ = AR.bf16(C, "Q0")
                    tt(DVE, q_[0:C, :], ps2[0:C, 0:C], msl[0:C, 0:C], ALU.mult, [kps2, k_cst], [kq_])
                    Q.append(q_[0:C, :])
                    k_Q.append(kq_)
                    Rpp.append([AR.bf16(C, "Ra"), AR.bf16(C, "Rb")])
                    PQpp.append([AR.bf16(2 * C, "PQa"), AR.bf16(2 * C, "PQb")])
                    r_, kr_ = Rpp[h][0]
                    tt(POOL, r_[0:C, :], idb[0:C, 0:C], s3[:, 0, :], ALU.subtract, [k_cb, ks_], [kr_])
                    R.append(r_[0:C, :])
                    k_R.append(kr_)
                    Pm.append(s3[:, 0, :])
                    k_P.append(ks_)
                for j in range(J + 1):
                    for h in range(8):
                        ps, kps = psF.next()
                        if j >= 1:
                            mm(ps[0:C, 2 * C:3 * C], Q[h], R[h], True, True, [k_Q[h], k_R[h]], [kps])
                        need_q = j < J
                        need_p = j + 1 < J
                        if need_q:
                            mm(ps[0:C, C:2 * C], Pm[h], Q[h], True, True, [k_P[h], k_Q[h]], [kps])
                        if need_p:
                            mm(ps[0:C, 0:C], Q[h], Pm[h], True, True, [k_Q[h], k_P[h]], [kps])
                        if j >= 1:
                            r_, kr_ = Rpp[h][j % 2]
                            tt(DVE, r_[0:C, :], ps[0:C, 2 * C:3 * C], R[h], ALU.add, [kps, k_R[h]], [kr_])
                            R[h], k_R[h] = r_[0:C, :], kr_
                        if need_q:
                            pq, kpq = PQpp[h][j % 2]
                            lo = 0 if need_p else C
                            cp(DVE, pq[0:C, lo:2 * C], ps[0:C, lo:2 * C], [kps], [kpq])
                            Pm[h], k_P[h] = pq[0:C, 0:C], kpq
                            Q[h], k_Q[h] = pq[0:C, C:2 * C], kpq
                if "sc1" in DBG:
                    S.op(POOL, (lambda o: (lambda e: e.memset(o, 0.0)))(yT[:, :, cs]), (), [k_yT[ck]])
                    AR.release(m_ck)
                    continue
                psw, kpsw = psF.next()
                for h in range(8):
                    hp, hl = h // 2, h % 2
                    pr = slice(64 * hl, 64 * hl + 64)
                    mm(psw[0:C, h * 64:(h + 1) * 64], kaT[pr, hp, cs], ss.STb[pr, i, hp, :], True, False, [k_kaT[hp], ss.k_STb[i]], [kpsw])
                    mm(psw[0:C, h * 64:(h + 1) * 64], sc[h][:, 2, :], tm[hp][0:C, hl * 64:(hl + 1) * 64], False, True, [k_sc[h], k_tm[hp]], [kpsw])
                wn, kwn = AR.bf16(512, "wn")
                amul(wn[0:C, :], psw[0:C, :], -1.0, [kpsw], [kwn])
                psu, kpsu = psF.next()
                for h in range(8):
                    mm(psu[0:C, h * 64:(h + 1) * 64], R[h], wn[0:C, h * 64:(h + 1) * 64], True, True, [k_R[h], kwn], [kpsu])
                ua, kua = AR.bf16(512, "ua")
                cp(DVE, ua[0:C, :], psu[0:C, :], [kpsu], [kua])
                psy, kpsy = psF.next()
                for h in range(8):
                    hp, hl = h // 2, h % 2
                    pr = slice(64 * hl, 64 * hl + 64)
                    o_ = psy[pr, hp * C:(hp + 1) * C]
                    mm(o_, ss.STb[pr, i, hp, :], rT[pr, hp, cs], True, False, [ss.k_STb[i], k_rT[hp]], [kpsy])
                    mm(o_, ua[0:C, h * 64:(h + 1) * 64], sc[h][:, 1, :], False, False, [kua, k_sc[h]], [kpsy])
                    mm(o_, tm[hp][0:C, hl * 64:(hl + 1) * 64], sc[h][:, 3, :], False, True, [k_tm[hp], k_sc[h]], [kpsy])
                acopy(yT[:, :, cs], psy[:, 0:4 * C].rearrange("p (c t) -> p c t", t=C), [kpsy], [k_yT[ck]])
                pss, kpss = psF.next()
                for h in range(8):
                    hp, hl = h // 2, h % 2
                    pr = slice(64 * hl, 64 * hl + 64)
                    o_ = pss[pr, hp * 64:(hp + 1) * 64]
                    mm(o_, tm[hp][0:C, 128 + hl * 64:128 + (hl + 1) * 64], ua[0:C, h * 64:(h + 1) * 64], True, False, [k_tm[hp], kua], [kpss])
                    mm(o_, tm[hp][0:C, 256 + hl * 64:256 + (hl + 1) * 64], tm[hp][0:C, hl * 64:(hl + 1) * 64], False, True, [k_tm[hp]], [kpss])
                stv = ss.ST[:, i].rearrange("p c v -> p (c v)")
                tt(DVE, stv, pss[:, 0:256], stv, ALU.add, [kpss, ss.k_ST[i]], [ss.k_ST[i]])
                tt(DVE, ss.ST[:, i], ss.ST[:, i], gam[:, :, ck:ck + 1].broadcast_to([128, 4, 64]), ALU.mult, [ss.k_ST[i], k_gam], [ss.k_ST[i]])
                acopy(ss.STb[:, i], ss.ST[:, i], [ss.k_ST[i]], [ss.k_STb[i]])
                AR.release(m_ck)
            if "ev3" in DBG:
                for hp in range(4):
                    acopy(ycat[:, hp, :], yT[:, hp, :], k_yT, [k_y[hp]])
                out_proj(n, sc_evout, k_evout, i, ycat, k_y)
                AR.release(m_base)
                return
            P_lw, P_lb = POFF[f"ln_w{i}"], POFF[f"ln_b{i}"]
            for hp in range(4):
                m_p = AR.mark()
                y = yT[:, hp, :]
                yb, kyb = AR.bf16(n, "yb")
                acopy(yb, y, k_yT, [kyb])
                ysq, kysq = AR.bf16(n, "ysq")
                act(ysq, y, AF.Square, k_yT, [kysq])
                ps1, kps1 = psF.next()
                mm(ps1[:, 0:n], blk1, yb, True, True, [k_cb, kyb], [kps1])
                ps2, kps2 = psF.next()
                mm(ps2[:, 0:n], blk1, ysq, True, True, [k_cb, kysq], [kps2])
                mean, kmean = AR.f32(n, "mean")
                amul(mean, ps1[:, 0:n], 1.0 / 64, [kps1], [kmean])
                m2, km2 = AR.f32(n, "m2")
                tt(POOL, m2, mean, mean, ALU.mult, [kmean], [km2])
                var, kvar = AR.f32(n, "var")
                stt(var, ps2[:, 0:n], 1.0 / 64, m2, ALU.mult, ALU.subtract, [kps2, km2], [kvar])
                act(var, var, AF.Sqrt, [kvar], [kvar], bias=GN_EPS)
                recip(var, var, [kvar], [kvar])
                dd, kdd = AR.f32(n, "dd")
                tt(POOL, dd, y, mean, ALU.subtract, k_yT + [kmean], [kdd])
                tt(DVE, dd, dd, var, ALU.mult, [kdd, kvar], [kdd])
                ts(POOL, dd, dd, pv[:, P_lw + hp:P_lw + hp + 1], pv[:, P_lb + hp:P_lb + hp + 1], ALU.mult, ALU.add, [kdd, k_pv], [kdd])
                tt(POOL, dd, dd, bon[:, hp, :], ALU.add, [kdd, k_bon[hp]], [kdd])
                tt(DVE, ycat[:, hp, :], dd, gg[:, hp, :], ALU.mult, [kdd, k_gg[hp]], [k_y[hp]])
                AR.release(m_p)
            out_proj(n, sc_evout, k_evout, i, ycat, k_y)
            AR.release(m_base)

        def odd_mixer(ss, n, C, layer):
            i = layer // 2
            nck = n // C
            rmsnorm(n, "mix_norm", layer * 8)
            m_base = AR.mark()

            chunkbuf = cbuf
            ocat, k_oc = chunkbuf(8, n, BF16, "ocat")
            qf, k_qf = chunkbuf(4, n, F32, "qf")
            kf, k_kf = chunkbuf(4, n, F32, "kf")
            vb, k_vb = chunkbuf(8, n, BF16, "vbo")
            sgl, k_sgl = chunkbuf(8, n, BF16, "sgl")
            alo, k_alo = AR.bf16(n, "alo_o")
            for eb in range(12):
                w, kw = wload(sl_p, sc_odin[i, eb], k_odin[(i, eb)])
                for sub in range(2):
                    ec = eb * 2 + sub
                    ps, kps = psF.next()
                    for kc in range(8):
                        mm(ps[:, 0:n], w[:, kc, sub * 128:(sub + 1) * 128], hT[:, kc, 0:n], kc == 0, kc == 7, [kw, k_h[kc]], [kps])
                    if ec < 4:
                        acopy(qf[:, ec, :], ps[:, 0:n], [kps], [k_qf[ec]])
                    elif ec < 8:
                        cp(DVE, kf[:, ec - 4, :], ps[:, 0:n], [kps], [k_kf[ec - 4]])
                    elif ec < 16:
                        acopy(vb[:, ec - 8, :], ps[:, 0:n], [kps], [k_vb[ec - 8]])
                    else:
                        act(sgl[:, ec - 16, :], ps[:, 0:n], AF.Silu, [kps], [k_sgl[ec - 16]])
            wt_, kwt = wload(sl_t, sc_odtail[i], k_odt[i])
            ps, kps = psF.next()
            for kc in range(8):
                mm(ps[0:16, 0:n], wt_[:, kc, :], hT[:, kc, 0:n], kc == 0, kc == 7, [kwt, k_h[kc]], [kps])
            acopy(alo[0:16, :], ps[0:16, 0:n], [kps], [k_alo])
            qt, k_qt = chunkbuf(4, n, BF16, "qt")
            kt, k_kt = chunkbuf(4, n, BF16, "kt")
            gam_ap, k_gam = AR.f32(4 * nck, "gamo")
            gam = gam_ap.rearrange("p (c k) -> p c k", k=nck)
            oT, k_oT = chunkbuf(8, n, F32, "oT")
            m_prep = AR.mark()
            for h in range(4):
                AR.release(m_prep)
                ps, kps = psF.next()
                mm(ps[:, 0:n], lgla_up[0:16, i, h * 128:(h + 1) * 128], alo[0:16, :], True, True, [k_lr, k_alo], [kps])
                e1, ke1 = AR.f32(n, "e1")
                act(e1, ps[:, 0:n], AF.Exp, [kps, k_pd], [ke1], bias=pd[:, 32 + i * 4 + h:32 + i * 4 + h + 1], scale=-1.0)
                act(e1, e1, AF.Ln, [ke1], [ke1], bias=1.0)
                cl, kcl = AR.f32(n, "cl")
                S.op(DVE, (lambda o, a_, b_: (lambda e: e.tensor_tensor_scan(o, a_, b_, 0.0, ALU.mult, ALU.add)))(cl, rmask[:, 0:n], e1),
                     [k_cst, ke1], [kcl])
                epos, kep = AR.f32(n, "epo")
                act(epos, cl, AF.Exp, [kcl], [kep], scale=-1.0 / 16)
                eneg, ken = AR.f32(n, "eno")
                act(eneg, cl, AF.Exp, [kcl], [ken], scale=1.0 / 16)
                stt(qt[:, h, :], qf[:, h, :], 128 ** -0.5, epos, ALU.mult, ALU.mult, [k_qf[h], kep], [k_qt[h]])
                tt(POOL, kt[:, h, :], kf[:, h, :], eneg, ALU.mult, [k_kf[h], ken], [k_kt[h]])
                cp(POOL, gam[:, h, :], epos.rearrange("p (k t) -> p k t", t=C)[:, :, C - 1], [kep], [k_gam])
            AR.release(m_prep)
            for ck in range(nck):
                cs = slice(ck * C, (ck + 1) * C)
                m_ck = AR.mark()
                tmg, k_tmg = [], []
                for h in range(4):
                    psb_, kpsb = psB.next()
                    tr(psb_[0:C, 0:128], vb[:, 2 * h, cs], idb, [k_vb[2 * h], k_cb], [kpsb])
                    tr(psb_[0:C, 128:256], vb[:, 2 * h + 1, cs], idb, [k_vb[2 * h + 1], k_cb], [kpsb])
                    tr(psb_[0:C, 256:384], kt[:, h, cs], idb, [k_kt[h], k_cb], [kpsb])
                    t_, kt_ = AR.bf16(384, "tmg")
                    if h % 2 == 0:
                        cp(DVE, t_[0:C, :], psb_[0:C, 0:384], [kpsb], [kt_])
                    else:
                        acopy(t_[0:C, :], psb_[0:C, 0:384], [kpsb], [kt_])
                    tmg.append(t_)
                    k_tmg.append(kt_)
                ps, kps = psF.next()
                psv = ps[0:C, :].rearrange("p (m t) -> p m t", t=128)
                for h in range(4):
                    mm(psv[:, h, 0:C], kt[:, h, cs], qt[:, h, cs], True, True, [k_kt[h], k_qt[h]], [kps])
                at_, kat = AR.bf16(4 * C, "at")
                at3 = at_[0:C, :].rearrange("p (m t) -> p m t", t=C)
                tt(DVE, at3, psv[:, :, 0:C], masku4[0:C, :, 0:C], ALU.mult, [kps, k_mu4], [kat])
                for hh in range(2):
                    pso, kpso = psF.next()
                    for hl in range(2):
                        h = 2 * hh + hl
                        for vc in range(2):
                            o_ = pso[:, (hl * 2 + vc) * C:(hl * 2 + vc + 1) * C]
                            mm(o_, ss.Gb[:, i, h, vc * 128:(vc + 1) * 128], qt[:, h, cs], True, False, [ss.k_Gb[i][hh], k_qt[h]], [kpso])
                            mm(o_, tmg[h][0:C, vc * 128:(vc + 1) * 128], at3[:, h, :], False, True, [k_tmg[h], kat], [kpso])
                    wr = [k_oT[4 * hh + c_] for c_ in range(4)]
                    if hh == 0:
                        acopy(oT[:, 4 * hh:4 * hh + 4, cs], pso[:, 0:4 * C].rearrange("p (c t) -> p c t", t=C), [kpso], wr)
                    else:
                        cp(DVE, oT[:, 4 * hh:4 * hh + 4, cs], pso[:, 0:4 * C].rearrange("p (c t) -> p c t", t=C), [kpso], wr)
                for hh in range(2):
                    pss, kpss = psF.next()
                    for hl in range(2):
                        h = 2 * hh + hl
                        mm(pss[:, hl * 256:(hl + 1) * 256], tmg[h][0:C, 256:384], tmg[h][0:C, 0:256], True, True, [k_tmg[h]], [kpss])
                    gv = ss.G[:, i, 2 * hh:2 * hh + 2, :]
                    kG = ss.k_G[i][hh]
                    tt(DVE, gv, pss[:, 0:512].rearrange("p (h v) -> p h v", v=256), gv, ALU.add, [kpss, kG], [kG])
                    tt(POOL, gv, gv, gam[:, 2 * hh:2 * hh + 2, ck:ck + 1].broadcast_to([128, 2, 256]), ALU.mult, [kG, k_gam], [kG])
                    acopy(ss.Gb[:, i, 2 * hh:2 * hh + 2, :], gv, [kG], [ss.k_Gb[i][hh]])
                AR.release(m_ck)
            P_gn = POFF[f"gla_n{i}"]
            for h in range(4):
                m_p = AR.mark()
                ps, kps = psF.next()
                for vc in range(2):
                    osq, kosq = AR.bf16(n, "osq")
                    act(osq, oT[:, 2 * h + vc, :], AF.Square, [k_oT[2 * h + vc]], [kosq])
                    mm(ps[:, 0:n], ones_b[:], osq, vc == 0, vc == 1, [k_ones, kosq], [kps])
                sd, ksd = AR.f32(n, "sdo")
                act(sd, ps[:, 0:n], AF.Sqrt, [kps], [ksd], bias=RMS_EPS, scale=1.0 / 256)
                recip(sd, sd, [ksd], [ksd])
                for vc in range(2):
                    on, kon = AR.f32(n, "on")
                    stt(on, oT[:, 2 * h + vc, :], pv[:, P_gn + vc:P_gn + vc + 1], sd, ALU.mult, ALU.mult, [k_oT[2 * h + vc], ksd, k_pv], [kon])
                    tt(POOL, ocat[:, 2 * h + vc, :], on, sgl[:, 2 * h + vc, :], ALU.mult, [kon, k_sgl[2 * h + vc]], [k_oc[2 * h + vc]])
                AR.release(m_p)
            out_proj(n, sc_odout, k_odout, i, ocat, k_oc)
            AR.release(m_base)

        def load_tile(src, t0, n):
            m0 = AR.mark()
            nsub = (n + 127) // 128
            stg = [AR.f32(D, "xin") for _ in range(2)]
            for sb in range(nsub):
                rows = min(128, n - sb * 128)
                st_, kst = stg[sb % 2]
                S.dma(SP, st_[0:rows, :], src[t0 + sb * 128:t0 + sb * 128 + rows, :], f"xin{sb % 2}", writes=[kst])
                for c in range(8):
                    ps, kps = psF.next()
                    tr(ps[:, 0:rows], st_[0:rows, c * 128:(c + 1) * 128], idf[0:rows, 0:rows], [kst, k_cst], [kps])
                    if c % 2 == 0:
                        acopy(xT[:, c, sb * 128:sb * 128 + rows], ps[:, 0:rows], [kps], [k_x[c]])
                    else:
                        cp(DVE, xT[:, c, sb * 128:sb * 128 + rows], ps[:, 0:rows], [kps], [k_x[c]])
            AR.release(m0)

        def store_tile(dst, t0, n):
            m0 = AR.mark()
            ps, kps = psF.next()
            sqr = [AR.bf16(n, "sqf") for _ in range(2)]
            for c in range(8):
                sq, ksq = sqr[c % 2]
                act(sq, xT[:, c, 0:n], AF.Square, [k_x[c]], [ksq])
                mm(ps[:, 0:n], ones_b[:], sq, c == 0, c == 7, [ksq, k_ones], [kps])
            sd, ksd = AR.f32(n, "sdf")
            act(sd, ps[:, 0:n], AF.Sqrt, [kps], [ksd], bias=RMS_EPS, scale=1.0 / D)
            recip(sd, sd, [ksd], [ksd])
            yf, k_yf = cbuf(8, n, F32, "yf")
            for c in range(8):
                stt(yf[:, c, :], xT[:, c, 0:n], pcol("final_norm", c), sd, ALU.mult, ALU.mult, [k_x[c], ksd, k_pv], [k_yf[c]])
            nsub = (n + 127) // 128
            stg = [AR.f32(D, "yout") for _ in range(2)]
            for sb in range(nsub):
                rows = min(128, n - sb * 128)
                st_, kst = stg[sb % 2]
                for c in range(8):
                    ps, kps = psF.next()
                    tr(ps[0:rows, 0:128], yf[:, c, sb * 128:sb * 128 + rows], idf, [k_yf[c], k_cst], [kps])
                    if c % 2 == 0:
                        acopy(st_[0:rows, c * 128:(c + 1) * 128], ps[0:rows, 0:128], [kps], [kst])
                    else:
                        cp(DVE, st_[0:rows, c * 128:(c + 1) * 128], ps[0:rows, 0:128], [kps], [kst])
                S.dma(SP, dst[t0 + sb * 128:t0 + sb * 128 + rows, :], st_[0:rows, :], f"yout{sb % 2}", reads=[kst])
            AR.release(m0)

        def store_states(ss):
            pre = ss.name
            m0 = AR.mark()
            for i in range(n_even):
                ps, kps = psF.next()
                tr(ps[0:14, 0:128], ss.shc[:, i, :], idf, [ss.k_shc[i], k_cst], [kps])
                stg, kst = AR.f32(128, "so1")
                acopy(stg[0:14, :], ps[0:14, 0:128], [kps], [kst])
                S.dma(SP, outs[pre + "_shift"][i].rearrange("(c p) -> c p", p=128), stg[0:14, :], None, final=True, reads=[kst])
                ps, kps = psF.next()
                tr(ps[0:8, 0:128], ss.cvc[:, i].rearrange("p c r -> p (c r)"), idf, [ss.k_cvc[i], k_cst], [kps])
                stg2, kst2 = AR.f32(128, "so2")
                acopy(stg2[0:8, :], ps[0:8, 0:128], [kps], [kst2])
                for c in range(4):
                    S.dma(SP, outs[pre + "_conv"][i][:, c * 128:(c + 1) * 128], stg2[2 * c:2 * c + 2, :], None, final=True, reads=[kst2])
                stg3, kst3 = AR.f32(512, "so3")
                for hp in range(4):
                    ps, kps = psF.next()
                    tr(ps[0:64, 0:128], ss.ST[:, i, hp, :], idf, [ss.k_ST[i], k_cst], [kps])
                    acopy(stg3[0:64, hp * 128:(hp + 1) * 128], ps[0:64, 0:128], [kps], [kst3])
                S.dma(SP, outs[pre + "_wkv"][i].rearrange("(hp hl) v k -> v hp hl k", hl=2),
                      stg3[0:64, :].rearrange("v (hp hl k) -> v hp hl k", hp=4, hl=2), None, final=True, reads=[kst3])
            for i in range(n_odd):
                for hh in range(2):
                    S.dma(SP, outs[pre + "_gla"][i, 2 * hh:2 * hh + 2].rearrange("h k v -> k h v"), ss.G[:, i, 2 * hh:2 * hh + 2, :], None, final=True, reads=[ss.k_G[i][hh]])
            AR.release(m0)

        def run_seq(ss, src, dst, Tlen, ntile, C):
            for t0 in range(0, Tlen, ntile):
                load_tile(src, t0, ntile)
                for layer in range(depth):
                    if "no_ffn" not in DBG:
                        ffn(ntile, layer, 0)
                    if layer % 2 == 0:
                        if "no_even" not in DBG:
                            even_mixer(ss, ntile, C, layer)
                    else:
                        if "no_odd" not in DBG:
                            odd_mixer(ss, ntile, C, layer)
                    if "no_ffn" not in DBG:
                        ffn(ntile, layer, 1)
                store_tile(dst, t0, ntile)
            if "no_states" not in DBG:
                store_states(ss)

        run_seq(seqs[0], xp, yp, T, min(NT, T), min(128, T))
        run_seq(seqs[1], xs_in, ys, TS, TS, min(128, TS))
        S.final_waits(SP, ["yout0", "yout1"] + sorted(S.final_sems))
        S.emit()
        build.stats = dict(n_ops={e: len(S.ops[e]) for e in S.engs}, arena_peak=AR.peak)
    return nc


_NC_CACHE = {}


def kernel(**inputs):
    inp = {k: np.asarray(v) for k, v in inputs.items()}
    xpr = inp["x_prompt"].astype(np.float32, copy=False)
    xsm = inp["x_sample"].astype(np.float32, copy=False)
    B, T, _ = xpr.shape
    BS, TS, _ = xsm.shape
    depth = inp["ffn_w_gate"].shape[0]
    n_cores = 8
    key = (T, TS, depth)
    if key not in _NC_CACHE:
        _NC_CACHE[key] = build(T, TS, depth=depth)
    nc = _NC_CACHE[key]
    pvec = _pack_params(inp)
    cst = _consts()
    zeros_p = np.zeros((T, D), np.float32)
    shared = {
        "ffn_w_gate": inp["ffn_w_gate"], "ffn_w_up": inp["ffn_w_up"], "ffn_w_down": inp["ffn_w_down"],
        "ev_w_in": inp["ev_w_in"], "ev_w_out": inp["ev_w_out"], "od_w_in": inp["od_w_in"], "od_w_out": inp["od_w_out"],
        "rwkv_w_up": inp["rwkv_w_up"], "rwkv_a_up": inp["rwkv_a_up"], "rwkv_g_up": inp["rwkv_g_up"],
        "gla_a_up": inp["gla_a_up"], "pvec": pvec, "consts": cst,
    }
    shared = {k: np.ascontiguousarray(v, dtype=np.float32) for k, v in shared.items()}
    in_maps = []
    for c in range(n_cores):
        m = dict(shared)
        m["xp"] = np.ascontiguousarray(xpr[c]) if c < B else zeros_p
        sb = c % BS
        m["xs"] = np.ascontiguousarray(xsm[sb])
        m["st_shift"] = np.ascontiguousarray(inp["state_rwkv_shift"][:, sb], dtype=np.float32)
        m["st_wkv"] = np.ascontiguousarray(inp["state_rwkv_wkv"][:, sb], dtype=np.float32)
        m["st_conv"] = np.ascontiguousarray(inp["state_conv"][:, sb], dtype=np.float32)
        m["st_gla"] = np.ascontiguousarray(inp["state_gla"][:, sb], dtype=np.float32)
        in_maps.append(m)
    res = run_bass_kernel_spmd(nc, in_maps, core_ids=list(range(n_cores)))
    R = res.results
    y_prompt = np.stack([R[c]["yp"] for c in range(B)], 0)
    y_sample = np.stack([R[c]["ys"] for c in range(BS)], 0)

    def gather(name, n):
        return np.stack([R[c][name] for c in range(n)], 1)
    return (y_prompt, y_sample,
            gather("p_shift", B), gather("p_wkv", B), gather("p_conv", B), gather("p_gla", B),
            gather("s_shift", BS), gather("s_wkv", BS), gather("s_conv", BS), gather("s_gla", BS))
```

```python
import math
from contextlib import ExitStack

import numpy as np
import concourse.bass as bass
import concourse.mybir as mybir
from concourse.bass_utils import run_bass_kernel_spmd

F32 = mybir.dt.float32
BF16 = mybir.dt.bfloat16
ALU = mybir.AluOpType
AF = mybir.ActivationFunctionType
PE, ACT, DVE, POOL, SP = "tensor", "scalar", "vector", "gpsimd", "sync"

D = 1024
DFF = 2816
NFC = 22
DEPTH = 4
A_COLS = 1792
EVEN_IN = 3328
ODD_IN = 3088
RMS_EPS = 1e-6
GN_EPS = 64e-5
CDEC = math.exp(-0.5)


class Key:
    __slots__ = ("name", "writer", "readers", "excl")

    def __init__(self, name="", excl=False):
        self.name = name
        self.writer = None
        self.readers = {}
        self.excl = excl


def _addr(readers, tok):
    cur = readers.get(tok[0])
    if cur is None or cur[1] < tok[1]:
        readers[tok[0]] = tok


class Sched:
    def __init__(self, nc, stack):
        self.nc = nc
        self.stack = stack
        self.engs = (PE, ACT, DVE, POOL, SP)
        self.ops = {e: [] for e in self.engs}
        self.count = {e: 0 for e in self.engs}
        self.sems = {e: stack.enter_context(nc.semaphore("s_" + e)) for e in self.engs}
        self.dma_sems = {}
        self.dma_count = {}
        self.final_sems = set()

    def sbuf(self, name, shape, dtype):
        return self.stack.enter_context(self.nc.sbuf_tensor(name, list(shape), dtype))

    def psum(self, name, shape, dtype):
        return self.stack.enter_context(self.nc.psum_tensor(name, list(shape), dtype))

    def _deps(self, eng, reads, writes):
        deps = []
        for k in reads:
            if k.writer is not None:
                deps.append(k.writer)
        same_ok = (eng == PE)
        for k in writes:
            w = k.writer
            if w is not None and (w[2] != eng or w[3] or not same_ok):
                deps.append(w)
            for r in k.readers.values():
                if r[2] != eng or r[3] or not same_ok:
                    deps.append(r)
        return deps

    def _commit(self, tok, reads, writes):
        for k in reads:
            _addr(k.readers, tok)
        for k in writes:
            k.writer = tok
            k.readers = {}

    @staticmethod
    def _excl(reads, writes):
        ex = [k for k in reads if k.excl]
        if not ex:
            return reads, writes
        return [k for k in reads if not k.excl], list(writes) + ex

    def op(self, eng, fn, reads=(), writes=()):
        reads, writes = self._excl(reads, writes)
        deps = self._deps(eng, reads, writes)
        self.count[eng] += 1
        tok = (eng, self.count[eng], eng, False)
        self.ops[eng].append((deps, fn, None))
        self._commit(tok, reads, writes)
        return tok

    def dma(self, queue, out, in_, sem, reads=(), writes=(), final=False):
        reads, writes = self._excl(reads, writes)
        deps = self._deps(queue, reads, writes)
        if sem is None:
            sem = f"u{len(self.dma_sems)}"
        if final:
            self.final_sems.add(sem)
        if sem not in self.dma_sems:
            self.dma_sems[sem] = self.stack.enter_context(self.nc.semaphore("d_" + sem))
            self.dma_count[sem] = 0
        self.dma_count[sem] += 16
        tok = ("dma:" + sem, self.dma_count[sem], queue, True)
        self.ops[queue].append((deps, lambda e: e.dma_start(out=out, in_=in_), sem))
        self._commit(tok, reads, writes)
        return tok

    def final_waits(self, queue, sems):
        deps = [("dma:" + s, self.dma_count[s], queue, True) for s in sems if s in self.dma_sems]
        self.ops[queue].append((deps, None, None))

    def _semh(self, sid):
        return self.dma_sems[sid[4:]] if sid.startswith("dma:") else self.sems[sid]

    def emit_engine(self, eng, handle):
        waited = {}
        for deps, fn, dsem in self.ops[eng]:
            need = {}
            for d in deps:
                if waited.get(d[0], 0) >= d[1]:
                    continue
                if need.get(d[0], 0) < d[1]:
                    need[d[0]] = d[1]
            for sid, val in need.items():
                handle.wait_ge(self._semh(sid), val)
                waited[sid] = val
            if fn is None:
                continue
            ins = fn(handle)
            if dsem is None:
                ins.then_inc(self.sems[eng], 1)
            else:
                ins.then_inc(self.dma_sems[dsem], 16)

    def emit(self):
        with self.nc.Block() as block:
            @block.tensor
            def _(e):
                self.emit_engine(PE, e)

            @block.scalar
            def _(e):
                self.emit_engine(ACT, e)

            @block.vector
            def _(e):
                self.emit_engine(DVE, e)

            @block.gpsimd
            def _(e):
                self.emit_engine(POOL, e)

            @block.sync
            def _(e):
                self.emit_engine(SP, e)


class Arena:
    def __init__(self, S, name, nbytes):
        self.t = S.sbuf(name, [128, nbytes // 4], F32)
        self.size = nbytes
        self.top = 0
        self.recs = []
        self.peak = 0

    def alloc(self, nbytes, name=""):
        nbytes = (nbytes + 63) // 64 * 64
        s, e = self.top, self.top + nbytes
        assert e <= self.size, f"arena overflow {name}: {e} > {self.size}"
        key = Key(name)
        keep = []
        for (rs, re_, rks) in self.recs:
            if rs < e and re_ > s:
                for rk in rks:
                    for r in rk.readers.values():
                        _addr(key.readers, r)
                    if rk.writer is not None:
                        _addr(key.readers, rk.writer)
                if rs >= s and re_ <= e:
                    continue
            keep.append((rs, re_, rks))
        rec = [key]
        keep.append((s, e, rec))
        self.recs = keep
        self.last_rec = rec
        self.top = e
        self.peak = max(self.peak, e)
        return self.t[:, s // 4:e // 4], key

    def subkeys(self, key, n):
        assert self.last_rec[0] is key
        ks = [Key(key.name + str(j)) for j in range(n)]
        for k_ in ks:
            k_.readers = dict(key.readers)
        self.last_rec[:] = ks
        return ks

    def f32(self, n, name=""):
        ap, k = self.alloc(n * 4, name)
        return ap[:, 0:n], k

    def bf16(self, n, name=""):
        ap, k = self.alloc(n * 2, name)
        return ap.bitcast(BF16)[:, 0:n], k

    def mark(self):
        return self.top

    def release(self, m):
        self.top = m


class Ring:
    def __init__(self, items):
        self.items = items
        self.i = 0

    def next(self):
        it = self.items[self.i % len(self.items)]
        self.i += 1
        return it


def _param_layout():
    off = {}
    r = 0

    def add(name, n):
        nonlocal r
        off[name] = r
        r += n
    add("ffn_norm", 64)
    add("mix_norm", 32)
    add("final_norm", 8)
    for i in range(2):
        for nm, n in (("mu", 14), ("w0", 4), ("a0", 4), ("k_k", 4), ("k_a", 4), ("r_k", 4),
                      ("ln_w", 4), ("ln_b", 4), ("conv_w", 12)):
            add(f"{nm}{i}", n)
    for i in range(2):
        add(f"gla_b{i}", 4)
        add(f"gla_n{i}", 2)
    return off, r


DBG = set()
POFF, PROWS = _param_layout()
NCONST = 128 * 5 + 512


def _pack_params(inp):
    rows = np.zeros((256, 128), np.float32)

    def put(name, arr):
        a = np.ascontiguousarray(arr, dtype=np.float32).reshape(-1, 128)
        rows[POFF[name]:POFF[name] + a.shape[0]] = a
    put("ffn_norm", inp["ffn_norm"])
    put("mix_norm", inp["mix_norm"])
    put("final_norm", inp["final_norm"])
    for i in range(2):
        put(f"mu{i}", inp["rwkv_mu"][i])
        put(f"w0{i}", inp["rwkv_w0"][i])
        put(f"a0{i}", inp["rwkv_a0"][i])
        put(f"k_k{i}", inp["rwkv_k_k"][i])
        put(f"k_a{i}", inp["rwkv_k_a"][i])
        put(f"r_k{i}", inp["rwkv_r_k"][i])
        put(f"ln_w{i}", inp["rwkv_ln_w"][i])
        put(f"ln_b{i}", inp["rwkv_ln_b"][i])
        put(f"conv_w{i}", inp["conv_w"][i])
        put(f"gla_b{i}", inp["gla_a_b"][i])
        put(f"gla_n{i}", inp["gla_norm"][i])
    return rows


def _consts():
    c = np.zeros((128, NCONST), np.float32)
    i = np.arange(128)[:, None]
    t = np.arange(128)[None, :]
    c[:, 0:128] = (i == t)
    c[:, 128:256] = (i < t)
    c[:, 256:384] = (i <= t)
    c[:, 384:512] = (i > t)
    c[:, 512:640] = ((i // 64) == (t // 64))
    rm = np.ones((128, 512), np.float32)
    rm[:, ::128] = 0.0
    c[:, 640:1152] = rm
    return c


def build(T, TS, NT=512, depth=DEPTH, arena_bytes=100 * 1024):
    nc = bass.Bass("TRN2", target_bir_lowering=False)
    n_even = (depth + 1) // 2
    n_odd = depth // 2

    def din(name, shape):
        return nc.dram_tensor(name, list(shape), F32, kind="ExternalInput").ap()

    def dout(name, shape):
        return nc.dram_tensor(name, list(shape), F32, kind="ExternalOutput").ap()

    def dscr(name, shape):
        return nc.dram_tensor(name, list(shape), BF16, kind="Internal").ap()

    xp = din("xp", [T, D])
    xs_in = din("xs", [TS, D])
    st_shift = din("st_shift", [2, A_COLS])
    st_wkv = din("st_wkv", [2, 8, 64, 64])
    st_conv = din("st_conv", [2, 2, 512])
    st_gla = din("st_gla", [2, 4, 128, 256])
    w_gate = din("ffn_w_gate", [depth, 2, D, DFF])
    w_up = din("ffn_w_up", [depth, 2, D, DFF])
    w_down = din("ffn_w_down", [depth, 2, DFF, D])
    ev_w_in = din("ev_w_in", [2, D, EVEN_IN])
    ev_w_out = din("ev_w_out", [2, D, D])
    od_w_in = din("od_w_in", [2, D, ODD_IN])
    od_w_out = din("od_w_out", [2, D, D])
    rwkv_w_up = din("rwkv_w_up", [2, 64, 512])
    rwkv_a_up = din("rwkv_a_up", [2, 64, 512])
    rwkv_g_up = din("rwkv_g_up", [2, 128, 512])
    gla_a_up = din("gla_a_up", [2, 16, 512])
    pvec = din("pvec", [256, 128])
    consts = din("consts", [128, NCONST])

    outs = {}
    for pre, in ("p", "s"):
        outs[pre + "_shift"] = dout(pre + "_shift", [2, A_COLS])
        outs[pre + "_wkv"] = dout(pre + "_wkv", [2, 8, 64, 64])
        outs[pre + "_conv"] = dout(pre + "_conv", [2, 2, 512])
        outs[pre + "_gla"] = dout(pre + "_gla", [2, 4, 128, 256])
    yp = dout("yp", [T, D])
    ys = dout("ys", [TS, D])

    sc_gu = dscr("sc_gu", [depth * 2, 11, 128, 2, 8, 256])
    sc_d = dscr("sc_d", [depth * 2, 8, 128, 11, 256])
    sc_evin = dscr("sc_evin", [2, 13, 128, 8, 256])
    sc_evout = dscr("sc_evout", [2, 4, 128, 8, 256])
    sc_odin = dscr("sc_odin", [2, 12, 128, 8, 256])
    sc_odtail = dscr("sc_odtail", [2, 128, 8, 16])
    sc_odout = dscr("sc_odout", [2, 4, 128, 8, 256])

    with ExitStack() as stack:
        S = Sched(nc, stack)

        cst = S.sbuf("cst", [128, NCONST], F32)
        k_cst = Key("cst")
        idf = cst[:, 0:128]
        cb = S.sbuf("cb", [128, 5, 128], BF16)
        k_cb = Key("cb")
        idb = cb[:, 0, :]
        blk1 = cb[:, 4, :]
        ones_b = S.sbuf("ones_b", [128, 128], BF16)
        k_ones = Key("ones")
        mask4 = S.sbuf("mask4", [128, 4, 128], F32)
        k_m4 = Key("m4")
        masku4 = S.sbuf("masku4", [128, 4, 128], F32)
        k_mu4 = Key("mu4")
        rmask = cst[:, 640:1152]
        msl = cst[:, 384:512]
        pv = S.sbuf("pv", [128, 256], F32)
        k_pv = Key("pv")
        pd = S.sbuf("pd", [128, 64], F32)
        k_pd = Key("pd")
        lw_up = S.sbuf("lw_up", [64, 2, 512], BF16)
        la_up = S.sbuf("la_up", [128, 2, 512], BF16)
        lg_up = S.sbuf("lg_up", [128, 2, 512], BF16)
        lgla_up = S.sbuf("lgla_up", [16, 2, 512], BF16)
        k_lr = Key("lowrank")

        xT = S.sbuf("xT", [128, 8, NT], F32)
        k_x = [Key(f"x{c}") for c in range(8)]
        hT = S.sbuf("hT", [128, 8, NT], BF16)
        k_h = [Key(f"h{c}") for c in range(8)]

        class SeqState:
            pass

        seqs = []
        for sname in ("p", "s"):
            ss = SeqState()
            ss.name = sname
            ss.ST = S.sbuf(sname + "ST", [128, 2, 4, 64], F32)
            ss.STb = S.sbuf(sname + "STb", [128, 2, 4, 64], BF16)
            ss.G = S.sbuf(sname + "G", [128, 2, 4, 256], F32)
            ss.Gb = S.sbuf(sname + "Gb", [128, 2, 4, 256], BF16)
            ss.shc = S.sbuf(sname + "shc", [128, 2, 14], F32)
            ss.cvc = S.sbuf(sname + "cvc", [128, 2, 4, 2], F32)
            ss.k_ST = [Key() for _ in range(2)]
            ss.k_STb = [Key() for _ in range(2)]
            ss.k_G = [[Key() for _ in range(2)] for _ in range(2)]
            ss.k_Gb = [[Key() for _ in range(2)] for _ in range(2)]
            ss.k_shc = [Key() for _ in range(2)]
            ss.k_cvc = [Key() for _ in range(2)]
            seqs.append(ss)

        def mkslots(name, n, shape):
            return Ring([(S.sbuf(f"{name}{i}", shape, BF16), Key(f"{name}{i}"), f"{name}{i}") for i in range(n)])
        sl_gu = mkslots("wgu", 2, [128, 2, 8, 256])
        sl_d = mkslots("wd", 3, [128, 11, 256])
        sl_p = mkslots("wp", 2, [128, 8, 256])
        sl_t = mkslots("wt", 1, [128, 8, 16])

        psF = Ring([(S.psum(f"psF{i}", [128, 512], F32), Key(f"psF{i}", excl=True)) for i in range(6)])
        psB = Ring([(S.psum(f"psB{i}", [128, 1024], BF16), Key(f"psB{i}", excl=True)) for i in range(2)])

        AR = Arena(S, "arena", (nc.sbuf_bytes_remaining - 1024) // 64 * 64)

        def mm(out, lhsT, rhs, start, stop, reads, writes):
            S.op(PE, lambda e: e.matmul(out, lhsT, rhs, start=start, stop=stop), reads, writes)

        def tr(out, in_, ident, reads, writes):
            S.op(PE, lambda e: e.transpose(out, in_, ident), reads, writes)

        def act(out, in_, func, reads, writes, bias=0.0, scale=1.0):
            S.op(ACT, lambda e: e.activation(out=out, in_=in_, func=func, bias=bias, scale=scale), reads, writes)

        def acopy(out, in_, reads, writes):
            S.op(ACT, lambda e: e.copy(out, in_), reads, writes)

        def amul(out, in_, m, reads, writes):
            S.op(ACT, lambda e: e.mul(out, in_, m), reads, writes)

        def tt(eng, out, in0, in1, op, reads, writes):
            S.op(eng, lambda e: e.tensor_tensor(out, in0, in1, op), reads, writes)

        def ts(eng, out, in0, s1, s2, op0, op1, reads, writes):
            if s2 is None:
                S.op(eng, lambda e: e.tensor_scalar(out, in0, s1, None, op0), reads, writes)
            else:
                S.op(eng, lambda e: e.tensor_scalar(out, in0, s1, s2, op0, op1), reads, writes)

        def stt(out, in0, sc, in1, op0, op1, reads, writes):
            S.op(DVE, lambda e: e.scalar_tensor_tensor(out, in0, sc, in1, op0, op1), reads, writes)

        def cp(eng, out, in_, reads, writes):
            S.op(eng, lambda e: e.tensor_copy(out, in_), reads, writes)

        def recip(out, in_, reads, writes):
            S.op(DVE, lambda e: e.reciprocal(out, in_), reads, writes)

        def mset(eng, ap, v, writes):
            S.op(eng, lambda e: e.memset(ap, v), (), writes)

        def pcol(name, j=0):
            r = POFF[name] + j
            return pv[:, r:r + 1]

        S.dma(SP, cst[:], consts[:, :], None, writes=[k_cst])
        for m in range(5):
            cp(DVE, cb[:, m, :], cst[:, m * 128:(m + 1) * 128], [k_cst], [k_cb])
        mset(POOL, ones_b[:], 1.0, [k_ones])
        for m in range(4):
            src = cst[:, 128:256] if m % 2 == 0 else cst[:, 256:384]
            cp(POOL, mask4[:, m, :], src, [k_cst], [k_m4])
            cp(POOL, masku4[:, m, :], cst[:, 256:384], [k_cst], [k_mu4])
        with_tmp = AR.mark()
        for half in range(2):
            stg, k_stg = AR.f32(128, "pstg")
            S.dma(SP, stg, pvec[half * 128:(half + 1) * 128, :], None, writes=[k_stg])
            ps, kps = psF.next()
            tr(ps[:, 0:128], stg, idf, [k_stg, k_cst], [kps])
            acopy(pv[:, half * 128:(half + 1) * 128], ps[:, 0:128], [kps], [k_pv])
        AR.release(with_tmp)
        for i in range(2):
            r = POFF[f"mu{i}"]
            ts(DVE, pd[:, i * 14:(i + 1) * 14], pv[:, r:r + 14], -1.0, 1.0, ALU.mult, ALU.add, [k_pv], [k_pd])
            r = POFF[f"gla_b{i}"]
            ts(DVE, pd[:, 32 + i * 4:32 + (i + 1) * 4], pv[:, r:r + 4], -1.0, None, ALU.mult, None, [k_pv], [k_pd])
        S.dma(POOL, lw_up[:], rwkv_w_up.rearrange("i k n -> k i n"), "c1", writes=[k_lr])
        S.dma(POOL, la_up[64:128], rwkv_a_up.rearrange("i k n -> k i n"), "c1", writes=[k_lr])
        S.dma(POOL, lg_up[:], rwkv_g_up.rearrange("i k n -> k i n"), "c1", writes=[k_lr])
        S.dma(POOL, lgla_up[:], gla_a_up.rearrange("i k n -> k i n"), "c1", writes=[k_lr])
        k_lr.writer = ("dma:c1", S.dma_count["c1"], POOL, True)

        wkeys = {}

        def conv_dma(layer, tag, out_ap, in_ap):
            k = Key(tag)
            S.dma(POOL, out_ap, in_ap, f"cv{layer}", writes=[k])
            wkeys.setdefault(layer, []).append(k)
            return k

        k_gu, k_d, k_evin, k_evout, k_odin, k_odt, k_odout = {}, {}, {}, {}, {}, {}, {}

        def conv_ffn(layer, j):
            f = layer * 2 + j
            for fb in range(11):
                ka = conv_dma(layer, "gu", sc_gu[f, fb, :, 0], w_gate[layer, j][:, fb * 256:(fb + 1) * 256].rearrange("(kc p) w -> p kc w", p=128))
                kb = conv_dma(layer, "gu", sc_gu[f, fb, :, 1], w_up[layer, j][:, fb * 256:(fb + 1) * 256].rearrange("(kc p) w -> p kc w", p=128))
                k_gu[(f, fb)] = [ka, kb]
            for db in range(4):
                for fh in range(2):
                    k_d[(f, db * 2 + fh)] = [conv_dma(layer, "d", sc_d[f, db * 2 + fh],
                                                      w_down[layer, j][fh * 1408:(fh + 1) * 1408, db * 256:(db + 1) * 256].rearrange("(fc p) w -> p fc w", p=128))]

        def conv_proj(layer, dst, src, nb, kd, i):
            for b in range(nb):
                kd[(i, b)] = [conv_dma(layer, "p", dst[i, b], src[i][:, b * 256:(b + 1) * 256].rearrange("(kc p) w -> p kc w", p=128))]

        for layer in range(depth if "no_cv" not in DBG else 0):
            i = layer // 2
            conv_ffn(layer, 0)
            if layer % 2 == 0:
                conv_proj(layer, sc_evin, ev_w_in, 13, k_evin, i)
                conv_proj(layer, sc_evout, ev_w_out, 4, k_evout, i)
            else:
                conv_proj(layer, sc_odin, od_w_in, 12, k_odin, i)
                k_odt[i] = [conv_dma(layer, "pt", sc_odtail[i], od_w_in[i][:, 3072:3088].rearrange("(kc p) w -> p kc w", p=128))]
                conv_proj(layer, sc_odout, od_w_out, 4, k_odout, i)
            conv_ffn(layer, 1)
        for layer, ks in wkeys.items():
            fin = ("dma:" + f"cv{layer}", S.dma_count[f"cv{layer}"], POOL, True)
            for k in ks:
                k.writer = fin

        def wload(ring, dram_ap, keys):
            t, k, sem = ring.next()
            S.dma(SP, t[:], dram_ap, sem, reads=keys, writes=[k])
            return t, k

        for ss in seqs:
            if ss.name == "p" or "no_sload" in DBG:
                for i in range(2):
                    mset(POOL, ss.ST[:, i], 0.0, [ss.k_ST[i]])
                    mset(POOL, ss.STb[:, i], 0.0, [ss.k_STb[i]])
                    for hh in range(2):
                        mset(POOL, ss.G[:, i, 2 * hh:2 * hh + 2], 0.0, [ss.k_G[i][hh]])
                        mset(POOL, ss.Gb[:, i, 2 * hh:2 * hh + 2], 0.0, [ss.k_Gb[i][hh]])
                    mset(POOL, ss.shc[:, i], 0.0, [ss.k_shc[i]])
                    mset(POOL, ss.cvc[:, i], 0.0, [ss.k_cvc[i]])
            else:
                m0 = AR.mark()
                for i in range(n_even):
                    stg, k_stg = AR.f32(128, "sstg")
                    S.dma(SP, stg[0:14, :], st_shift[i].rearrange("(c p) -> c p", p=128), None, writes=[k_stg])
                    ps, kps = psF.next()
                    tr(ps[:, 0:14], stg[0:14, :], idf[0:14, 0:14], [k_stg, k_cst], [kps])
                    acopy(ss.shc[:, i], ps[:, 0:14], [kps], [ss.k_shc[i]])
                    stg2, k_stg2 = AR.f32(128, "cstg")
                    S.dma(SP, stg2[0:8, :], st_conv[i].rearrange("r (c p) -> (r c) p", p=128), None, writes=[k_stg2])
                    ps, kps = psF.next()
                    tr(ps[:, 0:8], stg2[0:8, :], idf[0:8, 0:8], [k_stg2, k_cst], [kps])
                    acopy(ss.cvc[:, i].rearrange("p c r -> p r c"), ps[:, 0:8].rearrange("p (r c) -> p r c", c=4), [kps], [ss.k_cvc[i]])
                    for hp in range(4):
                        stg3, k_stg3 = AR.f32(128, "wstg")
                        S.dma(SP, stg3[0:64, :].rearrange("v (h k) -> v h k", h=2), st_wkv[i, 2 * hp:2 * hp + 2].rearrange("h v k -> v h k"), None, writes=[k_stg3])
                        ps, kps = psF.next()
                        tr(ps[:, 0:64], stg3[0:64, :], idf[0:64, 0:64], [k_stg3, k_cst], [kps])
                        acopy(ss.ST[:, i, hp, :], ps[:, 0:64], [kps], [ss.k_ST[i]])
                    cp(POOL, ss.STb[:, i], ss.ST[:, i], [ss.k_ST[i]], [ss.k_STb[i]])
                for i in range(n_odd):
                    for hh in range(2):
                        S.dma(SP, ss.G[:, i, 2 * hh:2 * hh + 2, :], st_gla[i, 2 * hh:2 * hh + 2].rearrange("h k v -> k h v"), None, writes=[ss.k_G[i][hh]])
                        cp(POOL, ss.Gb[:, i, 2 * hh:2 * hh + 2, :], ss.G[:, i, 2 * hh:2 * hh + 2, :], [ss.k_G[i][hh]], [ss.k_Gb[i][hh]])
                AR.release(m0)

        def cbuf(nchunk, width, dtype, name):
            eb = 2 if dtype == BF16 else 4
            ap, kall = AR.alloc(nchunk * width * eb, name)
            if dtype == BF16:
                ap = ap.bitcast(BF16)
            v = ap[:, 0:nchunk * width].rearrange("p (c t) -> p c t", t=width)
            return v, AR.subkeys(kall, nchunk)

        def rmsnorm(n, gname, gbase):
            m0 = AR.mark()
            ps, kps = psF.next()
            sqr = [AR.bf16(n, "sq") for _ in range(2)]
            for c in range(8):
                sq, ksq = sqr[c % 2]
                act(sq, xT[:, c, 0:n], AF.Square, [k_x[c]], [ksq])
                mm(ps[:, 0:n], ones_b[:], sq, c == 0, c == 7, [ksq, k_ones], [kps])
            rs, krs = AR.f32(n, "rs")
            act(rs, ps[:, 0:n], AF.Ln, [kps], [krs], bias=RMS_EPS, scale=1.0 / D)
            act(rs, rs, AF.Exp, [krs], [krs], scale=-0.5)
            for c in range(8):
                stt(hT[:, c, 0:n], xT[:, c, 0:n], pcol(gname, gbase + c), rs, ALU.mult, ALU.mult, [k_x[c], krs, k_pv], [k_h[c]])
            AR.release(m0)

        def ffn(n, layer, j):
            f = layer * 2 + j
            rmsnorm(n, "ffn_norm", f * 8)
            m0 = AR.mark()
            gT_ap, kgall = AR.alloc(NFC * n * 2, "gT")
            gT = gT_ap.bitcast(BF16)[:, 0:NFC * n].rearrange("p (c t) -> p c t", t=n)
            k_g = AR.subkeys(kgall, NFC)
            sar = Ring([AR.f32(n, "sA") for _ in range(2)])
            for fb in range(11):
                w, kw = wload(sl_gu, sc_gu[f, fb], k_gu[(f, fb)])
                for sub in range(2):
                    fc = fb * 2 + sub
                    psa, kpa = psF.next()
                    psb_, kpb = psF.next()
                    for kc in range(8):
                        mm(psa[:, 0:n], w[:, 0, kc, sub * 128:(sub + 1) * 128], hT[:, kc, 0:n], kc == 0, kc == 7, [kw, k_h[kc]], [kpa])
                    for kc in range(8):
                        mm(psb_[:, 0:n], w[:, 1, kc, sub * 128:(sub + 1) * 128], hT[:, kc, 0:n], kc == 0, kc == 7, [kw, k_h[kc]], [kpb])
                    sA, ksA = sar.next()
                    act(sA, psa[:, 0:n], AF.Silu, [kpa], [ksA])
                    tt(DVE, gT[:, fc, :], psb_[:, 0:n], sA, ALU.mult, [kpb, ksA], [k_g[fc]])
            for db in range(4):
                w0_, kw0 = wload(sl_d, sc_d[f, db * 2], k_d[(f, db * 2)])
                w1_, kw1 = wload(sl_d, sc_d[f, db * 2 + 1], k_d[(f, db * 2 + 1)])
                for sub in range(2):
                    dc = db * 2 + sub
                    ps, kps = psF.next()
                    for fc in range(NFC):
                        w_, kw_ = (w0_, kw0) if fc < 11 else (w1_, kw1)
                        mm(ps[:, 0:n], w_[:, fc % 11, sub * 128:(sub + 1) * 128], gT[:, fc, :], fc == 0, fc == NFC - 1, [kw_, k_g[fc]], [kps])
                    stt(xT[:, dc, 0:n], ps[:, 0:n], 0.5, xT[:, dc, 0:n], ALU.mult, ALU.add, [kps, k_x[dc]], [k_x[dc]])
            AR.release(m0)

        def out_proj(n, scr, kd, i, ycat, k_y):
            for db in range(4):
                w, kw = wload(sl_p, scr[i, db], kd[(i, db)])
                for sub in range(2):
                    dc = db * 2 + sub
                    ps, kps = psF.next()
                    for kc in range(8):
                        mm(ps[:, 0:n], w[:, kc, sub * 128:(sub + 1) * 128], ycat[:, kc, :], kc == 0, kc == 7, [kw, k_y[kc]], [kps])
                    tt(DVE, xT[:, dc, 0:n], ps[:, 0:n], xT[:, dc, 0:n], ALU.add, [kps, k_x[dc]], [k_x[dc]])

        def chunked(ap3, n):
            return ap3

        def even_mixer(ss, n, C, layer):
            i = layer // 2
            nck = n // C
            J = int(math.log2(C)) - 1
            rmsnorm(n, "mix_norm", layer * 8)
            m_base = AR.mark()
            ycat, k_y = cbuf(8, n, BF16, "ycat")
            th, k_th = AR.bf16(n, "th")
            alo, k_alo = AR.bf16(n, "alo")
            sg, k_sg = AR.bf16(n, "sg")
            rT, k_rT = cbuf(4, n, BF16, "rT")
            kaT, k_kaT = cbuf(4, n, BF16, "kaT")
            bT, k_bT = cbuf(4, n, BF16, "bT")
            kT, k_kT = cbuf(4, n, BF16, "kT")
            vb, k_vb = cbuf(4, n, BF16, "vb")
            bon, k_bon = cbuf(4, n, BF16, "bon")
            gg, k_gg = cbuf(4, n, BF16, "gg")
            gam_ap, k_gam = AR.f32(4 * nck, "gam")
            gam = gam_ap.rearrange("p (c k) -> p c k", k=nck)
            yT_ap, k_yTall = AR.alloc(4 * n * 4, "yT")
            yT = yT_ap[:, 0:4 * n].rearrange("p (c t) -> p c t", t=n)
            k_yT = AR.subkeys(k_yTall, nck)
            m_xs = AR.mark()
            mu0 = POFF[f"mu{i}"]

            def proj_chunk(eb):
                w, kw = wload(sl_p, sc_evin[i, eb], k_evin[(i, eb)])
                res = []
                for sub in range(2):
                    ps, kps = psF.next()
                    for kc in range(8):
                        mm(ps[:, 0:n], w[:, kc, sub * 128:(sub + 1) * 128], hT[:, kc, 0:n], kc == 0, kc == 7, [kw, k_h[kc]], [kps])
                    res.append((eb * 2 + sub, ps, kps))
                return res

            pbuf, k_pb = cbuf(8, n, F32, "pb")
            u, k_u = cbuf(4, n + 2, F32, "u")
            tmr = Ring([AR.f32(n, "tm1") for _ in range(3)])
            for eb in range(7, 13):
                for ec, ps, kps in proj_chunk(eb):
                    if ec < 22:
                        acopy(pbuf[:, ec - 14, :], ps[:, 0:n], [kps], [k_pb[ec - 14]])
                    else:
                        c = ec - 22
                        cp(POOL, u[:, c, 0:2], ss.cvc[:, i, c, :], [ss.k_cvc[i]], [k_u[c]])
                        tt(DVE, u[:, c, 2:n + 2], ps[:, 0:n], pbuf[:, 4 + c, :], ALU.mult, [kps, k_pb[4 + c]], [k_u[c]])
            cw = POFF[f"conv_w{i}"]
            for c in range(4):
                t1, kt1 = tmr.next()
                amul(t1, u[:, c, 0:n], pv[:, cw + c:cw + c + 1], [k_u[c], k_pv], [kt1])
                t2, kt2 = tmr.next()
                stt(t2, u[:, c, 1:n + 1], pv[:, cw + 4 + c:cw + 5 + c], t1, ALU.mult, ALU.add, [k_u[c], kt1, k_pv], [kt2])
                t3, kt3 = tmr.next()
                stt(t3, u[:, c, 2:n + 2], pv[:, cw + 8 + c:cw + 9 + c], t2, ALU.mult, ALU.add, [k_u[c], kt2, k_pv], [kt3])
                tt(POOL, ycat[:, 4 + c, :], t3, pbuf[:, c, :], ALU.mult, [kt3, k_pb[c]], [k_y[4 + c]])
                cp(POOL, ss.cvc[:, i, c, :], u[:, c, n:n + 2], [k_u[c]], [ss.k_cvc[i]])
            AR.release(m_xs)
            xs, k_xs = cbuf(12, n, F32, "xs")
            m_a = AR.mark()
            par = Ring([AR.f32(n + 1, "pa1") for _ in range(2)])
            tmr = Ring([AR.f32(n, "tm2") for _ in range(3)])
            for eb in range(0, 7):
                for ec, ps, kps in proj_chunk(eb):
                    pa1, kpa1 = par.next()
                    cp(POOL, pa1[:, 0:1], ss.shc[:, i, ec:ec + 1], [ss.k_shc[i]], [kpa1])
                    acopy(pa1[:, 1:n + 1], ps[:, 0:n], [kps], [kpa1])
                    t1, kt1 = tmr.next()
                    amul(t1, pa1[:, 0:n], pv[:, mu0 + ec:mu0 + ec + 1], [kpa1, k_pv], [kt1])
                    cp(POOL, ss.shc[:, i, ec:ec + 1], pa1[:, n:n + 1], [kpa1], [ss.k_shc[i]])
                    omm = pd[:, i * 14 + ec:i * 14 + ec + 1]
                    if ec < 12:
                        stt(xs[:, ec, :], pa1[:, 1:n + 1], omm, t1, ALU.mult, ALU.add, [kpa1, kt1, k_pd], [k_xs[ec]])
                    else:
                        t2, kt2 = tmr.next()
                        stt(t2, pa1[:, 1:n + 1], omm, t1, ALU.mult, ALU.add, [kpa1, kt1, k_pd], [kt2])
                        if ec == 12:
                            act(th[0:64, :], t2[0:64, :], AF.Tanh, [kt2], [k_th])
                            cp(POOL, alo[64:128, :], t2[64:128, :], [kt2], [k_alo])
                        else:
                            act(sg, t2, AF.Sigmoid, [kt2], [k_sg])
            AR.release(m_a)
            if "ev1" in DBG:
                for hp in range(4):
                    mset(POOL, ycat[:, hp, :], 0.0, [k_y[hp]])
                out_proj(n, sc_evout, k_evout, i, ycat, k_y)
                AR.release(m_base)
                return
            m_prep = AR.mark()
            P_w0, P_a0 = POFF[f"w0{i}"], POFF[f"a0{i}"]
            P_kk, P_ka, P_rk = POFF[f"k_k{i}"], POFF[f"k_a{i}"], POFF[f"r_k{i}"]
            for hp in range(4):
                AR.release(m_prep)
                xr, xk, xv = xs[:, hp, :], xs[:, 4 + hp, :], xs[:, 8 + hp, :]
                kxr, kxk, kxv = k_xs[hp], k_xs[4 + hp], k_xs[8 + hp]
                ps, kps = psF.next()
                mm(ps[:, 0:n], lw_up[0:64, i, hp * 128:(hp + 1) * 128], th[0:64, :], True, True, [k_lr, k_th], [kps])
                lw, klw = AR.f32(n, "lw")
                act(lw, ps[:, 0:n], AF.Sigmoid, [kps, k_pv], [klw], bias=pv[:, P_w0 + hp:P_w0 + hp + 1])
                ps, kps = psF.next()
                mm(ps[:, 0:n], la_up[64:128, i, hp * 128:(hp + 1) * 128], alo[64:128, :], True, True, [k_lr, k_alo], [kps])
                aa, kaa = AR.f32(n, "aa")
                act(aa, ps[:, 0:n], AF.Sigmoid, [kps, k_pv], [kaa], bias=pv[:, P_a0 + hp:P_a0 + hp + 1])
                ps, kps = psF.next()
                mm(ps[:, 0:n], lg_up[:, i, hp * 128:(hp + 1) * 128], sg, True, True, [k_lr, k_sg], [kps])
                acopy(gg[:, hp, :], ps[:, 0:n], [kps], [k_gg[hp]])
                kk2, kkk2 = AR.bf16(n, "kk2")
                act(kk2, xk, AF.Square, [kxk, k_pv], [kkk2], scale=pv[:, P_kk + hp:P_kk + hp + 1])
                ps, kps = psF.next()
                mm(ps[:, 0:n], blk1, kk2, True, True, [k_cb, kkk2], [kps])
                rn, krn = AR.f32(n, "rn")
                ts(DVE, rn, ps[:, 0:n], 1e-24, None, ALU.max, None, [kps], [krn])
                act(rn, rn, AF.Ln, [krn], [krn])
                act(rn, rn, AF.Exp, [krn], [krn], scale=-0.5)
                kap, kkap = rn, krn
                stt(kap, xk, pv[:, P_kk + hp:P_kk + hp + 1], rn, ALU.mult, ALU.mult, [kxk, krn, k_pv], [kkap])
                t1, kt1 = AR.f32(n, "t1")
                ts(POOL, t1, aa, -1.0, pv[:, P_ka + hp:P_ka + hp + 1], ALU.add, ALU.mult, [kaa, k_pv], [kt1])
                kef, kkef = t1, kt1
                stt(kef, t1, 1.0, xk, ALU.add, ALU.mult, [kt1, kxk], [kkef])
                bb, kbb = aa, kaa
                tt(POOL, bb, kap, aa, ALU.mult, [kkap, kaa], [kbb])
                rk, krk = AR.bf16(n, "rk")
                stt(rk, xr, pv[:, P_rk + hp:P_rk + hp + 1], kef, ALU.mult, ALU.mult, [kxr, kkef, k_pv], [krk])
                ps, kps = psF.next()
                mm(ps[:, 0:n], blk1, rk, True, True, [k_cb, krk], [kps])
                tt(DVE, bon[:, hp, :], ps[:, 0:n], xv, ALU.mult, [kps, kxv], [k_bon[hp]])
                cp(POOL, vb[:, hp, :], xv, [kxv], [k_vb[hp]])
                cum, kcum = AR.f32(n, "cum")
                S.op(DVE, (lambda o, a_, b_: (lambda e: e.tensor_tensor_scan(o, a_, b_, 0.0, ALU.mult, ALU.add)))(cum, rmask[:, 0:n], lw),
                     [k_cst, klw], [kcum])
                epos, kep = AR.f32(n, "epos")
                act(epos, cum, AF.Exp, [kcum], [kep], scale=-CDEC)
                eneg, ken = AR.f32(n, "eneg")
                act(eneg, cum, AF.Exp, [kcum], [ken], scale=CDEC)
                cml, kcml = lw, klw
                tt(POOL, cml, cum, lw, ALU.subtract, [kcum, klw], [kcml])
                act(cml, cml, AF.Exp, [kcml], [kcml], scale=-CDEC)
                tt(DVE, rT[:, hp, :], xr, epos, ALU.mult, [kxr, kep], [k_rT[hp]])
                tt(POOL, kaT[:, hp, :], kap, cml, ALU.mult, [kkap, kcml], [k_kaT[hp]])
                tt(DVE, bT[:, hp, :], bb, eneg, ALU.mult, [kbb, ken], [k_bT[hp]])
                tt(POOL, kT[:, hp, :], kef, eneg, ALU.mult, [kkef, ken], [k_kT[hp]])
                cp(POOL, gam[:, hp, :], epos.rearrange("p (k t) -> p k t", t=C)[:, :, C - 1], [kep], [k_gam])
            AR.release(m_xs)
            if "ev2" in DBG:
                for hp in range(4):
                    mset(POOL, ycat[:, hp, :], 0.0, [k_y[hp]])
                out_proj(n, sc_evout, k_evout, i, ycat, k_y)
                AR.release(m_base)
                return
            for ck in range(nck):
                cs = slice(ck * C, (ck + 1) * C)
                m_ck = AR.mark()
                tm, k_tm = [], []
                for hp in range(4):
                    psb_, kpsb = psB.next()
                    tr(psb_[0:C, 0:128], vb[:, hp, cs], idb, [k_vb[hp], k_cb], [kpsb])
                    tr(psb_[0:C, 128:256], bT[:, hp, cs], idb, [k_bT[hp], k_cb], [kpsb])
                    tr(psb_[0:C, 256:384], kT[:, hp, cs], idb, [k_kT[hp], k_cb], [kpsb])
                    t_, kt_ = AR.bf16(384, "tm")
                    if hp % 2 == 0:
                        cp(DVE, t_[0:C, :], psb_[0:C, 0:384], [kpsb], [kt_])
                    else:
                        acopy(t_[0:C, :], psb_[0:C, 0:384], [kpsb], [kt_])
                    tm.append(t_)
                    k_tm.append(kt_)
                sc, k_sc, Q, k_Q, R, k_R, Pm, k_P = [], [], [], [], [], [], [], []
                Rpp, PQpp = [], []
                for h in range(8):
                    hp, hl = h // 2, h % 2
                    pr = slice(64 * hl, 64 * hl + 64)
                    ps, kps = psF.next()
                    psv = ps[0:C, :].rearrange("p (m t) -> p m t", t=128)
                    mm(psv[:, 0, 0:C], bT[pr, hp, cs], kaT[pr, hp, cs], True, True, [k_bT[hp], k_kaT[hp]], [kps])
                    mm(psv[:, 1, 0:C], bT[pr, hp, cs], rT[pr, hp, cs], True, True, [k_bT[hp], k_rT[hp]], [kps])
                    mm(psv[:, 2, 0:C], kT[pr, hp, cs], kaT[pr, hp, cs], True, True, [k_kT[hp], k_kaT[hp]], [kps])
                    mm(psv[:, 3, 0:C], kT[pr, hp, cs], rT[pr, hp, cs], True, True, [k_kT[hp], k_rT[hp]], [kps])
                    s_, ks_ = AR.bf16(4 * C, "sc")
                    s3 = s_[0:C, :].rearrange("p (m t) -> p m t", t=C)
                    tt(DVE, s3, psv[:, :, 0:C], mask4[0:C, :, 0:C], ALU.mult, [kps, k_m4], [ks_])
                    sc.append(s3)
                    k_sc.append(ks_)
                    ps2, kps2 = psF.next()
                    mm(ps2[0:C, 0:C], kaT[pr, hp, cs], bT[pr, hp, cs], True, True, [k_kaT[hp], k_bT[hp]], [kps2])
                    q_, kq_ = AR.bf16(C, "Q0")
                    tt(DVE, q_[0:C, :], ps2[0:C, 0:C], msl[0:C, 0:C], ALU.mult, [kps2, k_cst], [kq_])
                    Q.append(q_[0:C, :])
                    k_Q.append(kq_)
                    Rpp.append([AR.bf16(C, "Ra"), AR.bf16(C, "Rb")])
                    PQpp.append([AR.bf16(2 * C, "PQa"), AR.bf16(2 * C, "PQb")])
                    r_, kr_ = Rpp[h][0]
                    tt(POOL, r_[0:C, :], idb[0:C, 0:C], s3[:, 0, :], ALU.subtract, [k_cb, ks_], [kr_])
                    R.append(r_[0:C, :])
                    k_R.append(kr_)
                    Pm.append(s3[:, 0, :])
                    k_P.append(ks_)
                for j in range(J + 1):
                    for h in range(8):
                        ps, kps = psF.next()
                        if j >= 1:
                            mm(ps[0:C, 2 * C:3 * C], Q[h], R[h], True, True, [k_Q[h], k_R[h]], [kps])
                        need_q = j < J
                        need_p = j + 1 < J
                        if need_q:
                            mm(ps[0:C, C:2 * C], Pm[h], Q[h], True, True, [k_P[h], k_Q[h]], [kps])
                        if need_p:
                            mm(ps[0:C, 0:C], Q[h], Pm[h], True, True, [k_Q[h], k_P[h]], [kps])
                        if j >= 1:
                            r_, kr_ = Rpp[h][j % 2]
                            tt(DVE, r_[0:C, :], ps[0:C, 2 * C:3 * C], R[h], ALU.add, [kps, k_R[h]], [kr_])
                            R[h], k_R[h] = r_[0:C, :], kr_
                        if need_q:
                            pq, kpq = PQpp[h][j % 2]
                            lo = 0 if need_p else C
                            cp(DVE, pq[0:C, lo:2 * C], ps[0:C, lo:2 * C], [kps], [kpq])
                            Pm[h], k_P[h] = pq[0:C, 0:C], kpq
                            Q[h], k_Q[h] = pq[0:C, C:2 * C], kpq
                if "sc1" in DBG:
                    S.op(POOL, (lambda o: (lambda e: e.memset(o, 0.0)))(yT[:, :, cs]), (), [k_yT[ck]])
                    AR.release(m_ck)
                    continue
                psw, kpsw = psF.next()
                for h in range(8):
                    hp, hl = h // 2, h % 2
                    pr = slice(64 * hl, 64 * hl + 64)
                    mm(psw[0:C, h * 64:(h + 1) * 64], kaT[pr, hp, cs], ss.STb[pr, i, hp, :], True, False, [k_kaT[hp], ss.k_STb[i]], [kpsw])
                    mm(psw[0:C, h * 64:(h + 1) * 64], sc[h][:, 2, :], tm[hp][0:C, hl * 64:(hl + 1) * 64], False, True, [k_sc[h], k_tm[hp]], [kpsw])
                wn, kwn = AR.bf16(512, "wn")
                amul(wn[0:C, :], psw[0:C, :], -1.0, [kpsw], [kwn])
                psu, kpsu = psF.next()
                for h in range(8):
                    mm(psu[0:C, h * 64:(h + 1) * 64], R[h], wn[0:C, h * 64:(h + 1) * 64], True, True, [k_R[h], kwn], [kpsu])
                ua, kua = AR.bf16(512, "ua")
                cp(DVE, ua[0:C, :], psu[0:C, :], [kpsu], [kua])
                psy, kpsy = psF.next()
                for h in range(8):
                    hp, hl = h // 2, h % 2
                    pr = slice(64 * hl, 64 * hl + 64)
                    o_ = psy[pr, hp * C:(hp + 1) * C]
                    mm(o_, ss.STb[pr, i, hp, :], rT[pr, hp, cs], True, False, [ss.k_STb[i], k_rT[hp]], [kpsy])
                    mm(o_, ua[0:C, h * 64:(h + 1) * 64], sc[h][:, 1, :], False, False, [kua, k_sc[h]], [kpsy])
                    mm(o_, tm[hp][0:C, hl * 64:(hl + 1) * 64], sc[h][:, 3, :], False, True, [k_tm[hp], k_sc[h]], [kpsy])
                acopy(yT[:, :, cs], psy[:, 0:4 * C].rearrange("p (c t) -> p c t", t=C), [kpsy], [k_yT[ck]])
                pss, kpss = psF.next()
                for h in range(8):
                    hp, hl = h // 2, h % 2
                    pr = slice(64 * hl, 64 * hl + 64)
                    o_ = pss[pr, hp * 64:(hp + 1) * 64]
                    mm(o_, tm[hp][0:C, 128 + hl * 64:128 + (hl + 1) * 64], ua[0:C, h * 64:(h + 1) * 64], True, False, [k_tm[hp], kua], [kpss])
                    mm(o_, tm[hp][0:C, 256 + hl * 64:256 + (hl + 1) * 64], tm[hp][0:C, hl * 64:(hl + 1) * 64], False, True, [k_tm[hp]], [kpss])
                stv = ss.ST[:, i].rearrange("p c v -> p (c v)")
                tt(DVE, stv, pss[:, 0:256], stv, ALU.add, [kpss, ss.k_ST[i]], [ss.k_ST[i]])
                tt(DVE, ss.ST[:, i], ss.ST[:, i], gam[:, :, ck:ck + 1].broadcast_to([128, 4, 64]), ALU.mult, [ss.k_ST[i], k_gam], [ss.k_ST[i]])
                acopy(ss.STb[:, i], ss.ST[:, i], [ss.k_ST[i]], [ss.k_STb[i]])
                AR.release(m_ck)
            if "ev3" in DBG:
                for hp in range(4):
                    acopy(ycat[:, hp, :], yT[:, hp, :], k_yT, [k_y[hp]])
                out_proj(n, sc_evout, k_evout, i, ycat, k_y)
                AR.release(m_base)
                return
            P_lw, P_lb = POFF[f"ln_w{i}"], POFF[f"ln_b{i}"]
            for hp in range(4):
                m_p = AR.mark()
                y = yT[:, hp, :]
                yb, kyb = AR.bf16(n, "yb")
                acopy(yb, y, k_yT, [kyb])
                ysq, kysq = AR.bf16(n, "ysq")
                act(ysq, y, AF.Square, k_yT, [kysq])
                ps1, kps1 = psF.next()
                mm(ps1[:, 0:n], blk1, yb, True, True, [k_cb, kyb], [kps1])
                ps2, kps2 = psF.next()
                mm(ps2[:, 0:n], blk1, ysq, True, True, [k_cb, kysq], [kps2])
                mean, kmean = AR.f32(n, "mean")
                amul(mean, ps1[:, 0:n], 1.0 / 64, [kps1], [kmean])
                m2, km2 = AR.f32(n, "m2")
                tt(POOL, m2, mean, mean, ALU.mult, [kmean], [km2])
                var, kvar = AR.f32(n, "var")
                stt(var, ps2[:, 0:n], 1.0 / 64, m2, ALU.mult, ALU.subtract, [kps2, km2], [kvar])
                act(var, var, AF.Ln, [kvar], [kvar], bias=GN_EPS)
                act(var, var, AF.Exp, [kvar], [kvar], scale=-0.5)
                dd, kdd = AR.f32(n, "dd")
                tt(POOL, dd, y, mean, ALU.subtract, k_yT + [kmean], [kdd])
                tt(DVE, dd, dd, var, ALU.mult, [kdd, kvar], [kdd])
                ts(POOL, dd, dd, pv[:, P_lw + hp:P_lw + hp + 1], pv[:, P_lb + hp:P_lb + hp + 1], ALU.mult, ALU.add, [kdd, k_pv], [kdd])
                tt(POOL, dd, dd, bon[:, hp, :], ALU.add, [kdd, k_bon[hp]], [kdd])
                tt(DVE, ycat[:, hp, :], dd, gg[:, hp, :], ALU.mult, [kdd, k_gg[hp]], [k_y[hp]])
                AR.release(m_p)
            out_proj(n, sc_evout, k_evout, i, ycat, k_y)
            AR.release(m_base)

        def odd_mixer(ss, n, C, layer):
            i = layer // 2
            nck = n // C
            rmsnorm(n, "mix_norm", layer * 8)
            m_base = AR.mark()

            chunkbuf = cbuf
            ocat, k_oc = chunkbuf(8, n, BF16, "ocat")
            qf, k_qf = chunkbuf(4, n, F32, "qf")
            kf, k_kf = chunkbuf(4, n, F32, "kf")
            vb, k_vb = chunkbuf(8, n, BF16, "vbo")
            sgl, k_sgl = chunkbuf(8, n, BF16, "sgl")
            alo, k_alo = AR.bf16(n, "alo_o")
            for eb in range(12):
                w, kw = wload(sl_p, sc_odin[i, eb], k_odin[(i, eb)])
                for sub in range(2):
                    ec = eb * 2 + sub
                    ps, kps = psF.next()
                    for kc in range(8):
                        mm(ps[:, 0:n], w[:, kc, sub * 128:(sub + 1) * 128], hT[:, kc, 0:n], kc == 0, kc == 7, [kw, k_h[kc]], [kps])
                    if ec < 4:
                        acopy(qf[:, ec, :], ps[:, 0:n], [kps], [k_qf[ec]])
                    elif ec < 8:
                        cp(DVE, kf[:, ec - 4, :], ps[:, 0:n], [kps], [k_kf[ec - 4]])
                    elif ec < 16:
                        acopy(vb[:, ec - 8, :], ps[:, 0:n], [kps], [k_vb[ec - 8]])
                    else:
                        act(sgl[:, ec - 16, :], ps[:, 0:n], AF.Silu, [kps], [k_sgl[ec - 16]])
            wt_, kwt = wload(sl_t, sc_odtail[i], k_odt[i])
            ps, kps = psF.next()
            for kc in range(8):
                mm(ps[0:16, 0:n], wt_[:, kc, :], hT[:, kc, 0:n], kc == 0, kc == 7, [kwt, k_h[kc]], [kps])
            acopy(alo[0:16, :], ps[0:16, 0:n], [kps], [k_alo])
            qt, k_qt = chunkbuf(4, n, BF16, "qt")
            kt, k_kt = chunkbuf(4, n, BF16, "kt")
            gam_ap, k_gam = AR.f32(4 * nck, "gamo")
            gam = gam_ap.rearrange("p (c k) -> p c k", k=nck)
            oT, k_oT = chunkbuf(8, n, F32, "oT")
            m_prep = AR.mark()
            for h in range(4):
                AR.release(m_prep)
                ps, kps = psF.next()
                mm(ps[:, 0:n], lgla_up[0:16, i, h * 128:(h + 1) * 128], alo[0:16, :], True, True, [k_lr, k_alo], [kps])
                e1, ke1 = AR.f32(n, "e1")
                act(e1, ps[:, 0:n], AF.Exp, [kps, k_pd], [ke1], bias=pd[:, 32 + i * 4 + h:32 + i * 4 + h + 1], scale=-1.0)
                act(e1, e1, AF.Ln, [ke1], [ke1], bias=1.0)
                cl, kcl = AR.f32(n, "cl")
                S.op(DVE, (lambda o, a_, b_: (lambda e: e.tensor_tensor_scan(o, a_, b_, 0.0, ALU.mult, ALU.add)))(cl, rmask[:, 0:n], e1),
                     [k_cst, ke1], [kcl])
                epos, kep = AR.f32(n, "epo")
                act(epos, cl, AF.Exp, [kcl], [kep], scale=-1.0 / 16)
                eneg, ken = AR.f32(n, "eno")
                act(eneg, cl, AF.Exp, [kcl], [ken], scale=1.0 / 16)
                stt(qt[:, h, :], qf[:, h, :], 128 ** -0.5, epos, ALU.mult, ALU.mult, [k_qf[h], kep], [k_qt[h]])
                tt(POOL, kt[:, h, :], kf[:, h, :], eneg, ALU.mult, [k_kf[h], ken], [k_kt[h]])
                cp(POOL, gam[:, h, :], epos.rearrange("p (k t) -> p k t", t=C)[:, :, C - 1], [kep], [k_gam])
            AR.release(m_prep)
            for ck in range(nck):
                cs = slice(ck * C, (ck + 1) * C)
                m_ck = AR.mark()
                tmg, k_tmg = [], []
                for h in range(4):
                    psb_, kpsb = psB.next()
                    tr(psb_[0:C, 0:128], vb[:, 2 * h, cs], idb, [k_vb[2 * h], k_cb], [kpsb])
                    tr(psb_[0:C, 128:256], vb[:, 2 * h + 1, cs], idb, [k_vb[2 * h + 1], k_cb], [kpsb])
                    tr(psb_[0:C, 256:384], kt[:, h, cs], idb, [k_kt[h], k_cb], [kpsb])
                    t_, kt_ = AR.bf16(384, "tmg")
                    if h % 2 == 0:
                        cp(DVE, t_[0:C, :], psb_[0:C, 0:384], [kpsb], [kt_])
                    else:
                        acopy(t_[0:C, :], psb_[0:C, 0:384], [kpsb], [kt_])
                    tmg.append(t_)
                    k_tmg.append(kt_)
                ps, kps = psF.next()
                psv = ps[0:C, :].rearrange("p (m t) -> p m t", t=128)
                for h in range(4):
                    mm(psv[:, h, 0:C], kt[:, h, cs], qt[:, h, cs], True, True, [k_kt[h], k_qt[h]], [kps])
                at_, kat = AR.bf16(4 * C, "at")
                at3 = at_[0:C, :].rearrange("p (m t) -> p m t", t=C)
                tt(DVE, at3, psv[:, :, 0:C], masku4[0:C, :, 0:C], ALU.mult, [kps, k_mu4], [kat])
                for hh in range(2):
                    pso, kpso = psF.next()
                    for hl in range(2):
                        h = 2 * hh + hl
                        for vc in range(2):
                            o_ = pso[:, (hl * 2 + vc) * C:(hl * 2 + vc + 1) * C]
                            mm(o_, ss.Gb[:, i, h, vc * 128:(vc + 1) * 128], qt[:, h, cs], True, False, [ss.k_Gb[i][hh], k_qt[h]], [kpso])
                            mm(o_, tmg[h][0:C, vc * 128:(vc + 1) * 128], at3[:, h, :], False, True, [k_tmg[h], kat], [kpso])
                    wr = [k_oT[4 * hh + c_] for c_ in range(4)]
                    if hh == 0:
                        acopy(oT[:, 4 * hh:4 * hh + 4, cs], pso[:, 0:4 * C].rearrange("p (c t) -> p c t", t=C), [kpso], wr)
                    else:
                        cp(DVE, oT[:, 4 * hh:4 * hh + 4, cs], pso[:, 0:4 * C].rearrange("p (c t) -> p c t", t=C), [kpso], wr)
                for hh in range(2):
                    pss, kpss = psF.next()
                    for hl in range(2):
                        h = 2 * hh + hl
                        mm(pss[:, hl * 256:(hl + 1) * 256], tmg[h][0:C, 256:384], tmg[h][0:C, 0:256], True, True, [k_tmg[h]], [kpss])
                    gv = ss.G[:, i, 2 * hh:2 * hh + 2, :]
                    kG = ss.k_G[i][hh]
                    tt(DVE, gv, pss[:, 0:512].rearrange("p (h v) -> p h v", v=256), gv, ALU.add, [kpss, kG], [kG])
                    tt(POOL, gv, gv, gam[:, 2 * hh:2 * hh + 2, ck:ck + 1].broadcast_to([128, 2, 256]), ALU.mult, [kG, k_gam], [kG])
                    acopy(ss.Gb[:, i, 2 * hh:2 * hh + 2, :], gv, [kG], [ss.k_Gb[i][hh]])
                AR.release(m_ck)
            P_gn = POFF[f"gla_n{i}"]
            for h in range(4):
                m_p = AR.mark()
                ps, kps = psF.next()
                for vc in range(2):
                    osq, kosq = AR.bf16(n, "osq")
                    act(osq, oT[:, 2 * h + vc, :], AF.Square, [k_oT[2 * h + vc]], [kosq])
                    mm(ps[:, 0:n], ones_b[:], osq, vc == 0, vc == 1, [k_ones, kosq], [kps])
                sd, ksd = AR.f32(n, "sdo")
                act(sd, ps[:, 0:n], AF.Ln, [kps], [ksd], bias=RMS_EPS, scale=1.0 / 256)
                act(sd, sd, AF.Exp, [ksd], [ksd], scale=-0.5)
                for vc in range(2):
                    on, kon = AR.f32(n, "on")
                    stt(on, oT[:, 2 * h + vc, :], pv[:, P_gn + vc:P_gn + vc + 1], sd, ALU.mult, ALU.mult, [k_oT[2 * h + vc], ksd, k_pv], [kon])
                    tt(POOL, ocat[:, 2 * h + vc, :], on, sgl[:, 2 * h + vc, :], ALU.mult, [kon, k_sgl[2 * h + vc]], [k_oc[2 * h + vc]])
                AR.release(m_p)
            out_proj(n, sc_odout, k_odout, i, ocat, k_oc)
            AR.release(m_base)

        def load_tile(src, t0, n):
            m0 = AR.mark()
            nsub = (n + 127) // 128
            stg = [AR.f32(D, "xin") for _ in range(2)]
            for sb in range(nsub):
                rows = min(128, n - sb * 128)
                st_, kst = stg[sb % 2]
                S.dma(SP, st_[0:rows, :], src[t0 + sb * 128:t0 + sb * 128 + rows, :], f"xin{sb % 2}", writes=[kst])
                for c in range(8):
                    ps, kps = psF.next()
                    tr(ps[:, 0:rows], st_[0:rows, c * 128:(c + 1) * 128], idf[0:rows, 0:rows], [kst, k_cst], [kps])
                    if c % 2 == 0:
                        acopy(xT[:, c, sb * 128:sb * 128 + rows], ps[:, 0:rows], [kps], [k_x[c]])
                    else:
                        cp(DVE, xT[:, c, sb * 128:sb * 128 + rows], ps[:, 0:rows], [kps], [k_x[c]])
            AR.release(m0)

        def store_tile(dst, t0, n):
            m0 = AR.mark()
            ps, kps = psF.next()
            sqr = [AR.bf16(n, "sqf") for _ in range(2)]
            for c in range(8):
                sq, ksq = sqr[c % 2]
                act(sq, xT[:, c, 0:n], AF.Square, [k_x[c]], [ksq])
                mm(ps[:, 0:n], ones_b[:], sq, c == 0, c == 7, [ksq, k_ones], [kps])
            sd, ksd = AR.f32(n, "sdf")
            act(sd, ps[:, 0:n], AF.Ln, [kps], [ksd], bias=RMS_EPS, scale=1.0 / D)
            act(sd, sd, AF.Exp, [ksd], [ksd], scale=-0.5)
            yf, k_yf = cbuf(8, n, F32, "yf")
            for c in range(8):
                stt(yf[:, c, :], xT[:, c, 0:n], pcol("final_norm", c), sd, ALU.mult, ALU.mult, [k_x[c], ksd, k_pv], [k_yf[c]])
            nsub = (n + 127) // 128
            stg = [AR.f32(D, "yout") for _ in range(2)]
            for sb in range(nsub):
                rows = min(128, n - sb * 128)
                st_, kst = stg[sb % 2]
                for c in range(8):
                    ps, kps = psF.next()
                    tr(ps[0:rows, 0:128], yf[:, c, sb * 128:sb * 128 + rows], idf, [k_yf[c], k_cst], [kps])
                    if c % 2 == 0:
                        acopy(st_[0:rows, c * 128:(c + 1) * 128], ps[0:rows, 0:128], [kps], [kst])
                    else:
                        cp(DVE, st_[0:rows, c * 128:(c + 1) * 128], ps[0:rows, 0:128], [kps], [kst])
                S.dma(SP, dst[t0 + sb * 128:t0 + sb * 128 + rows, :], st_[0:rows, :], f"yout{sb % 2}", reads=[kst])
            AR.release(m0)

        def store_states(ss):
            pre = ss.name
            m0 = AR.mark()
            for i in range(n_even):
                ps, kps = psF.next()
                tr(ps[0:14, 0:128], ss.shc[:, i, :], idf, [ss.k_shc[i], k_cst], [kps])
                stg, kst = AR.f32(128, "so1")
                acopy(stg[0:14, :], ps[0:14, 0:128], [kps], [kst])
                S.dma(SP, outs[pre + "_shift"][i].rearrange("(c p) -> c p", p=128), stg[0:14, :], None, final=True, reads=[kst])
                ps, kps = psF.next()
                tr(ps[0:8, 0:128], ss.cvc[:, i].rearrange("p c r -> p (c r)"), idf, [ss.k_cvc[i], k_cst], [kps])
                stg2, kst2 = AR.f32(128, "so2")
                acopy(stg2[0:8, :], ps[0:8, 0:128], [kps], [kst2])
                for c in range(4):
                    S.dma(SP, outs[pre + "_conv"][i][:, c * 128:(c + 1) * 128], stg2[2 * c:2 * c + 2, :], None, final=True, reads=[kst2])
                stg3, kst3 = AR.f32(512, "so3")
                for hp in range(4):
                    ps, kps = psF.next()
                    tr(ps[0:64, 0:128], ss.ST[:, i, hp, :], idf, [ss.k_ST[i], k_cst], [kps])
                    acopy(stg3[0:64, hp * 128:(hp + 1) * 128], ps[0:64, 0:128], [kps], [kst3])
                S.dma(SP, outs[pre + "_wkv"][i].rearrange("(hp hl) v k -> v hp hl k", hl=2),
                      stg3[0:64, :].rearrange("v (hp hl k) -> v hp hl k", hp=4, hl=2), None, final=True, reads=[kst3])
            for i in range(n_odd):
                for hh in range(2):
                    S.dma(SP, outs[pre + "_gla"][i, 2 * hh:2 * hh + 2].rearrange("h k v -> k h v"), ss.G[:, i, 2 * hh:2 * hh + 2, :], None, final=True, reads=[ss.k_G[i][hh]])
            AR.release(m0)

        def run_seq(ss, src, dst, Tlen, ntile, C):
            for t0 in range(0, Tlen, ntile):
                load_tile(src, t0, ntile)
                for layer in range(depth):
                    if "no_ffn" not in DBG:
                        ffn(ntile, layer, 0)
                    if layer % 2 == 0:
                        if "no_even" not in DBG:
                            even_mixer(ss, ntile, C, layer)
                    else:
                        if "no_odd" not in DBG:
                            odd_mixer(ss, ntile, C, layer)
                    if "no_ffn" not in DBG:
                        ffn(ntile, layer, 1)
                store_tile(dst, t0, ntile)
            if "no_states" not in DBG:
                store_states(ss)

        run_seq(seqs[0], xp, yp, T, min(NT, T), min(128, T))
        run_seq(seqs[1], xs_in, ys, TS, TS, min(128, TS))
        S.final_waits(SP, ["yout0", "yout1"] + sorted(S.final_sems))
        S.emit()
        build.stats = dict(n_ops={e: len(S.ops[e]) for e in S.engs}, arena_peak=AR.peak)
    return nc


_NC_CACHE = {}


def kernel(**inputs):
    inp = {k: np.asarray(v) for k, v in inputs.items()}
    xpr = inp["x_prompt"].astype(np.float32, copy=False)
    xsm = inp["x_sample"].astype(np.float32, copy=False)
    B, T, _ = xpr.shape
    BS, TS, _ = xsm.shape
    depth = inp["ffn_w_gate"].shape[0]
    n_cores = 8
    key = (T, TS, depth)
    if key not in _NC_CACHE:
        _NC_CACHE[key] = build(T, TS, depth=depth)
    nc = _NC_CACHE[key]
    pvec = _pack_params(inp)
    cst = _consts()
    zeros_p = np.zeros((T, D), np.float32)
    shared = {
        "ffn_w_gate": inp["ffn_w_gate"], "ffn_w_up": inp["ffn_w_up"], "ffn_w_down": inp["ffn_w_down"],
        "ev_w_in": inp["ev_w_in"], "ev_w_out": inp["ev_w_out"], "od_w_in": inp["od_w_in"], "od_w_out": inp["od_w_out"],
        "rwkv_w_up": inp["rwkv_w_up"], "rwkv_a_up": inp["rwkv_a_up"], "rwkv_g_up": inp["rwkv_g_up"],
        "gla_a_up": inp["gla_a_up"], "pvec": pvec, "consts": cst,
    }
    shared = {k: np.ascontiguousarray(v, dtype=np.float32) for k, v in shared.items()}
    in_maps = []
    for c in range(n_cores):
        m = dict(shared)
        m["xp"] = np.ascontiguousarray(xpr[c]) if c < B else zeros_p
        sb = c % BS
        m["xs"] = np.ascontiguousarray(xsm[sb])
        m["st_shift"] = np.ascontiguousarray(inp["state_rwkv_shift"][:, sb], dtype=np.float32)
        m["st_wkv"] = np.ascontiguousarray(inp["state_rwkv_wkv"][:, sb], dtype=np.float32)
        m["st_conv"] = np.ascontiguousarray(inp["state_conv"][:, sb], dtype=np.float32)
        m["st_gla"] = np.ascontiguousarray(inp["state_gla"][:, sb], dtype=np.float32)
        in_maps.append(m)
    res = run_bass_kernel_spmd(nc, in_maps, core_ids=list(range(n_cores)))
    R = res.results
    y_prompt = np.stack([R[c]["yp"] for c in range(B)], 0)
    y_sample = np.stack([R[c]["ys"] for c in range(BS)], 0)

    def gather(name, n):
        return np.stack([R[c][name] for c in range(n)], 1)
    return (y_prompt, y_sample,
            gather("p_shift", B), gather("p_wkv", B), gather("p_conv", B), gather("p_gla", B),
            gather("s_shift", BS), gather("s_wkv", BS), gather("s_conv", BS), gather("s_gla", BS))
```
